# Optimizing a Trainium2 kernel written in Bass

```python
import functools
import jax, jax.numpy as jnp
from jax import lax
import numpy as np

D_MODEL = 1024
BATCH = 16
SEQ = 2048
DEPTH = 1
DEC_BATCH = 128
DEC_SEQ = 8
PAST_LEN = 8192
PAGE_SIZE = 128

N_META = 16
MLA_HEADS = 8
Q_LORA = 384
KV_LORA = 256
QK_NOPE = 64
QK_ROPE = 32
V_HEAD = 64
ROPE_THETA = 10000.0
CONV_DIM = D_MODEL // 2
CONV_K = 3
D_FF = 4 * D_MODEL
Q_BLOCK = 128
LN_EPS = 1e-5
RMS_EPS = 1e-6
ALPHA = (2.0 * DEPTH) ** 0.25
BETA = (8.0 * DEPTH) ** -0.25
SM_SCALE = (QK_NOPE + QK_ROPE) ** -0.5
SPLIT_SIZES = (Q_LORA, KV_LORA, QK_ROPE, CONV_DIM, CONV_DIM, CONV_DIM, D_MODEL, D_MODEL)
D_IN = Q_LORA + KV_LORA + QK_ROPE + 3 * CONV_DIM + 2 * D_MODEL

kernel_name = "hybrid_conv_mla_deepnorm_step"


def layer_norm(x, g, b):
    xf = x.astype(jnp.float32)
    mu = jnp.mean(xf, axis=-1, keepdims=True)
    var = jnp.mean(jnp.square(xf - mu), axis=-1, keepdims=True)
    y = (xf - mu) * lax.rsqrt(var + LN_EPS) * g.astype(jnp.float32) + b.astype(jnp.float32)
    return y.astype(x.dtype)


def rms_norm(x, g):
    xf = x.astype(jnp.float32)
    y = xf * lax.rsqrt(jnp.mean(jnp.square(xf), axis=-1, keepdims=True) + RMS_EPS) * g.astype(jnp.float32)
    return y.astype(x.dtype)


def rope(x, pos):
    half = QK_ROPE // 2
    inv_freq = ROPE_THETA ** (-2.0 * jnp.arange(half, dtype=jnp.float32) / QK_ROPE)
    ang = pos.astype(jnp.float32)[:, None] * inv_freq[None, :]
    cos = jnp.cos(ang)[:, None, :]
    sin = jnp.sin(ang)[:, None, :]
    xf = x.astype(jnp.float32)
    x1, x2 = xf[..., :half], xf[..., half:]
    return jnp.concatenate([x1 * cos - x2 * sin, x2 * cos + x1 * sin], axis=-1).astype(x.dtype)


def mixer_projections(h, pos, w):
    b, s, _ = h.shape
    offsets = [int(v) for v in np.cumsum(SPLIT_SIZES)[:-1]]
    z = h @ w["w_in"]
    cq, ckv_raw, kpe_raw, gate_b, gate_c, conv_h, gate_cv, gate_ml = jnp.split(z, offsets, axis=-1)
    cq = rms_norm(cq, w["q_norm_g"])
    q = (cq @ w["w_uq"]).reshape(b, s, MLA_HEADS, QK_NOPE + QK_ROPE)
    q_nope, q_pe = q[..., :QK_NOPE], q[..., QK_NOPE:]
    q_pe = rope(q_pe, pos)
    q_lat = jnp.einsum("bshn,chn->bshc", q_nope, w["w_uk"])
    ckv = rms_norm(ckv_raw, w["kv_norm_g"])
    kpe = rope(kpe_raw[:, :, None, :], pos)[:, :, 0, :]
    u = gate_c * conv_h
    g_conv = jax.nn.sigmoid(gate_cv)
    g_mla = jax.nn.sigmoid(gate_ml)
    return q_lat, q_pe, ckv, kpe, gate_b, u, g_conv, g_mla


def short_conv(u, prev, conv_w):
    s = u.shape[1]
    full = jnp.concatenate([prev, u], axis=1)
    out = sum(conv_w[k] * full[:, k:k + s] for k in range(CONV_K))
    return out, full[:, -(CONV_K - 1):]


def attend_prompt(q_lat, q_pe, ckv, kpe):
    b, L = q_lat.shape[0], q_lat.shape[1]
    lp = -(-L // Q_BLOCK) * Q_BLOCK
    pad = lp - L
    nb = lp // Q_BLOCK
    q_lat_p = jnp.pad(q_lat, ((0, 0), (0, pad), (0, 0), (0, 0)))
    q_pe_p = jnp.pad(q_pe, ((0, 0), (0, pad), (0, 0), (0, 0)))
    ckv_p = jnp.pad(ckv, ((0, 0), (0, pad), (0, 0)))
    kpe_p = jnp.pad(kpe, ((0, 0), (0, pad), (0, 0)))
    k_pos = jnp.arange(lp)
    qb = q_lat_p.reshape(b, nb, Q_BLOCK, MLA_HEADS, KV_LORA).swapaxes(0, 1)
    pb = q_pe_p.reshape(b, nb, Q_BLOCK, MLA_HEADS, QK_ROPE).swapaxes(0, 1)
    starts = jnp.arange(nb) * Q_BLOCK

    def block(args):
        ql, qp, start = args
        q_pos = start + jnp.arange(Q_BLOCK)
        sc = (jnp.einsum("bqhc,bkc->bhqk", ql, ckv_p) + jnp.einsum("bqhr,bkr->bhqk", qp, kpe_p)).astype(jnp.float32) * SM_SCALE
        sc = jnp.where(k_pos[None, :] <= q_pos[:, None], sc, -jnp.inf)
        p = jax.nn.softmax(sc, axis=-1).astype(ckv_p.dtype)
        return jnp.einsum("bhqk,bkc->bqhc", p, ckv_p)

    o = lax.map(block, (qb, pb, starts))
    return o.swapaxes(0, 1).reshape(b, lp, MLA_HEADS, KV_LORA)[:, :L]


def attend_sample(q_lat, q_pe, ckv_new, kpe_new, ckv_pool, kpe_pool, page_table):
    db = page_table.shape[0]
    n = q_lat.shape[1]
    past_ckv = ckv_pool[page_table].reshape(db, -1, KV_LORA)
    past_kpe = kpe_pool[page_table].reshape(db, -1, QK_ROPE)
    past = past_ckv.shape[1]
    s_past = jnp.einsum("bqhc,bkc->bhqk", q_lat, past_ckv) + jnp.einsum("bqhr,bkr->bhqk", q_pe, past_kpe)
    s_new = jnp.einsum("bqhc,bkc->bhqk", q_lat, ckv_new) + jnp.einsum("bqhr,bkr->bhqk", q_pe, kpe_new)
    s_new = jnp.where(jnp.tril(jnp.ones((n, n), dtype=bool)), s_new.astype(jnp.float32), -jnp.inf)
    sc = jnp.concatenate([s_past.astype(jnp.float32), s_new], axis=-1) * SM_SCALE
    p = jax.nn.softmax(sc, axis=-1).astype(ckv_new.dtype)
    return (jnp.einsum("bhqk,bkc->bqhc", p[..., :past], past_ckv)
            + jnp.einsum("bhqk,bkc->bqhc", p[..., past:], ckv_new))


def layer_forward(h, pos, conv_prev, attend, w):
    b, s, _ = h.shape
    q_lat, q_pe, ckv, kpe, gate_b, u, g_conv, g_mla = mixer_projections(h, pos, w)
    conv_out, conv_state = short_conv(u, conv_prev, w["conv_w"])
    y_conv = (gate_b * conv_out) @ w["w_conv_out"]
    o_lat = attend(q_lat, q_pe, ckv, kpe)
    v = jnp.einsum("bshc,chv->bshv", o_lat, w["w_uv"]).reshape(b, s, MLA_HEADS * V_HEAD)
    y_mla = v @ w["w_o_mla"]
    mix = (g_conv * y_conv + g_mla * y_mla) @ w["w_o"]
    h1 = layer_norm(ALPHA * h + mix, w["ln1_g"], w["ln1_b"])
    ff = jnp.square(jax.nn.relu(h1 @ w["w_up"])) @ w["w_down"]
    h2 = layer_norm(ALPHA * h1 + ff, w["ln2_g"], w["ln2_b"])
    return h2, ckv, kpe, conv_state


def setup_inputs(seed: int = 0) -> dict:
    key = jax.random.key(seed)
    ks = jax.random.split(key, 32)
    f32 = jnp.float32
    nrm = lambda k, shape, scale: jax.random.normal(k, shape, f32) * scale
    n_pages = PAST_LEN // PAGE_SIZE
    n_used = DEC_BATCH * n_pages
    n_pool = n_used + n_used // 4
    page_table = jax.random.permutation(ks[5], n_pool)[:n_used].reshape(DEC_BATCH, n_pages).astype(jnp.int32)
    return {
        "x_prompt": nrm(ks[0], (BATCH, SEQ, D_MODEL), 1.0),
        "x_sample": nrm(ks[1], (DEC_BATCH, DEC_SEQ, D_MODEL), 1.0),
        "cache_ckv": nrm(ks[2], (DEPTH, n_pool, PAGE_SIZE, KV_LORA), 1.0),
        "cache_kpe": nrm(ks[3], (DEPTH, n_pool, PAGE_SIZE, QK_ROPE), 1.0),
        "state_conv": nrm(ks[4], (DEPTH, DEC_BATCH, CONV_K - 1, CONV_DIM), 1.0),
        "page_table": page_table,
        "meta_tokens": nrm(ks[6], (N_META, D_MODEL), 1.0),
        "ln_emb_g": 1.0 + nrm(ks[7], (D_MODEL,), 0.02),
        "ln_emb_b": nrm(ks[8], (D_MODEL,), 0.02),
        "w_in": nrm(ks[9], (DEPTH, D_MODEL, D_IN), D_MODEL ** -0.5),
        "q_norm_g": 1.0 + nrm(ks[10], (DEPTH, Q_LORA), 0.02),
        "w_uq": nrm(ks[11], (DEPTH, Q_LORA, MLA_HEADS * (QK_NOPE + QK_ROPE)), Q_LORA ** -0.5),
        "kv_norm_g": 1.0 + nrm(ks[12], (DEPTH, KV_LORA), 0.02),
        "w_uk": nrm(ks[13], (DEPTH, KV_LORA, MLA_HEADS, QK_NOPE), KV_LORA ** -0.5),
        "w_uv": nrm(ks[14], (DEPTH, KV_LORA, MLA_HEADS, V_HEAD), KV_LORA ** -0.5),
        "w_o_mla": nrm(ks[15], (DEPTH, MLA_HEADS * V_HEAD, D_MODEL), (MLA_HEADS * V_HEAD) ** -0.5),
        "conv_w": nrm(ks[16], (DEPTH, CONV_K, CONV_DIM), CONV_K ** -0.5),
        "w_conv_out": nrm(ks[17], (DEPTH, CONV_DIM, D_MODEL), CONV_DIM ** -0.5),
        "w_o": nrm(ks[18], (DEPTH, D_MODEL, D_MODEL), BETA * D_MODEL ** -0.5),
        "ln1_g": 1.0 + nrm(ks[19], (DEPTH, D_MODEL), 0.02),
        "ln1_b": nrm(ks[20], (DEPTH, D_MODEL), 0.02),
        "w_up": nrm(ks[21], (DEPTH, D_MODEL, D_FF), D_MODEL ** -0.5),
        "w_down": nrm(ks[22], (DEPTH, D_FF, D_MODEL), BETA * D_FF ** -0.5),
        "ln2_g": 1.0 + nrm(ks[23], (DEPTH, D_MODEL), 0.02),
        "ln2_b": nrm(ks[24], (DEPTH, D_MODEL), 0.02),
    }


def reference(x_prompt, x_sample, cache_ckv, cache_kpe, state_conv, page_table, meta_tokens,
              ln_emb_g, ln_emb_b, w_in, q_norm_g, w_uq, kv_norm_g, w_uk, w_uv, w_o_mla,
              conv_w, w_conv_out, w_o, ln1_g, ln1_b, w_up, w_down, ln2_g, ln2_b):
    bp, sp, _ = x_prompt.shape
    meta = jnp.broadcast_to(meta_tokens[None].astype(x_prompt.dtype), (bp, N_META, D_MODEL))
    hp = layer_norm(jnp.concatenate([meta, x_prompt], axis=1), ln_emb_g, ln_emb_b)
    hs = layer_norm(x_sample, ln_emb_g, ln_emb_b)
    pos_p = jnp.arange(N_META + sp)
    past_len = page_table.shape[1] * PAGE_SIZE
    pos_s = past_len + jnp.arange(x_sample.shape[1])
    conv_zero = jnp.zeros((bp, CONV_K - 1, CONV_DIM), dtype=hp.dtype)
    ckv_p_l, kpe_p_l, conv_p_l, ckv_s_l, kpe_s_l, conv_s_l = [], [], [], [], [], []
    for l in range(DEPTH):
        w = {
            "w_in": w_in[l], "q_norm_g": q_norm_g[l], "w_uq": w_uq[l], "kv_norm_g": kv_norm_g[l],
            "w_uk": w_uk[l], "w_uv": w_uv[l], "w_o_mla": w_o_mla[l], "conv_w": conv_w[l],
            "w_conv_out": w_conv_out[l], "w_o": w_o[l], "ln1_g": ln1_g[l], "ln1_b": ln1_b[l],
            "w_up": w_up[l], "w_down": w_down[l], "ln2_g": ln2_g[l], "ln2_b": ln2_b[l],
        }
        hp, ckv_p, kpe_p, conv_p = layer_forward(hp, pos_p, conv_zero, attend_prompt, w)
        attend_s = functools.partial(attend_sample, ckv_pool=cache_ckv[l], kpe_pool=cache_kpe[l], page_table=page_table)
        hs, ckv_s, kpe_s, conv_s = layer_forward(hs, pos_s, state_conv[l], attend_s, w)
        ckv_p_l.append(ckv_p); kpe_p_l.append(kpe_p); conv_p_l.append(conv_p)
        ckv_s_l.append(ckv_s); kpe_s_l.append(kpe_s); conv_s_l.append(conv_s)
    y_prompt = hp[:, N_META:]
    y_sample = hs
    return (y_prompt, y_sample,
            jnp.stack(ckv_p_l), jnp.stack(kpe_p_l), jnp.stack(conv_p_l),
            jnp.stack(ckv_s_l), jnp.stack(kpe_s_l), jnp.stack(conv_s_l))
```

```python
import numpy as np
import concourse.bass as bass
import concourse.mybir as mybir
from concourse.bass_utils import run_bass_kernel_spmd

F32 = mybir.dt.float32
BF16 = mybir.dt.bfloat16
I32 = mybir.dt.int32
ALU = mybir.AluOpType
AF = mybir.ActivationFunctionType

ALPHA = 2.0 ** 0.25
SM_SCALE = 96.0 ** -0.5
D_IN = 4256


class Trk:
    def __init__(self, nc):
        self.nc = nc
        self.engs = {"pe": nc.tensor, "act": nc.scalar, "dve": nc.vector,
                     "pool": nc.gpsimd, "sp": nc.sync}
        self.sem = {k: nc.alloc_semaphore("s_" + k) for k in self.engs}
        self.cnt = {k: 0 for k in self.engs}
        self.seen = {k: {} for k in self.engs}
        self.lastw = {}
        self.readers = {}
        self.dsem = {}
        self.dcnt = {}
        self.waited = {}
        self.nins = 0

    def _semh(self, sk):
        return self.sem[sk] if sk in self.sem else self.dsem[sk]

    def _deps(self, e, r, w):
        best = {}
        def add(t):
            sk, v = t
            if sk == "pe" and e == "pe":
                return
            if v > best.get(sk, 0):
                best[sk] = v
        for b in r:
            if b in self.lastw:
                add(self.lastw[b])
        for b in w:
            if b in self.lastw:
                add(self.lastw[b])
            for t in self.readers.get(b, ()):
                add(t)
        for sk, v in best.items():
            if sk in self.dsem:
                cand = [x for x in self.waited.setdefault(sk, []) if x >= v]
                v = min(cand) if cand else self.dcnt[sk]
                if v not in self.waited[sk]:
                    self.waited[sk].append(v)
            if self.seen[e].get(sk, 0) < v:
                self.engs[e].wait_ge(self._semh(sk), v)
                self.seen[e][sk] = v

    def _rec(self, tok, r, w):
        for b in r:
            lst = self.readers.setdefault(b, [])
            lst[:] = [t for t in lst if t[0] != tok[0]]
            lst.append(tok)
        for b in w:
            self.lastw[b] = tok
            self.readers[b] = []

    def op(self, e, fn, r=(), w=()):
        ps = [b for b in r if b.startswith("ps")]
        if ps:
            r = [b for b in r if not b.startswith("ps")]
            w = list(w) + ps
        self._deps(e, r, w)
        ins = fn(self.engs[e])
        self.cnt[e] += 1
        ins.then_inc(self.sem[e], 1)
        self._rec((e, self.cnt[e]), r, w)
        self.nins += 1
        return ins

    def dma(self, q, fn, semkey, r=(), w=()):
        if semkey not in self.dsem:
            self.dsem[semkey] = self.nc.alloc_semaphore("d_" + semkey)
            self.dcnt[semkey] = 0
        self._deps(q, r, w)
        ins = fn(self.engs[q])
        self.dcnt[semkey] += 16
        ins.then_inc(self.dsem[semkey], 16)
        self._rec((semkey, self.dcnt[semkey]), r, w)
        self.nins += 1
        return ins

    def _waitall(self, e):
        for sk, v in list(self.cnt.items()) + list(self.dcnt.items()):
            if sk == e or v == 0:
                continue
            if self.seen[e].get(sk, 0) < v:
                self.engs[e].wait_ge(self._semh(sk), v)
                self.seen[e][sk] = v

    def barrier(self):
        for e in self.engs:
            self._waitall(e)

    def finish(self, q):
        self._waitall(q)


class Rot:
    def __init__(self, ids):
        self.ids = list(ids)
        self.i = 0

    def next(self):
        v = self.ids[self.i % len(self.ids)]
        self.i += 1
        return v


class _Stop(Exception):
    pass


def build(NSEQ, SEQ, NPG, NPOOL, debug=(), stop=None):
    TT = 256
    NT = SEQ // TT
    L = SEQ + 16
    NBLK = SEQ // 128
    NCHV = NBLK + 1
    nc = bass.Bass("TRN2", target_bir_lowering=False)
    T = Trk(nc)

    def din(name, shape, dt=F32):
        return nc.dram_tensor(name, list(shape), dt, kind="ExternalInput").ap()

    def dout(name, shape, dt=F32):
        return nc.dram_tensor(name, list(shape), dt, kind="ExternalOutput").ap()

    def sb(name, shape, dt=F32):
        return nc.alloc_sbuf_tensor(name, list(shape), dt)

    x_p = din("x_p", [NSEQ, SEQ, 1024]); x_s = din("x_s", [128, 1024]); meta = din("meta", [16, 1024])
    cckv = din("cckv", [NPOOL * 128, 256]); ckpe = din("ckpe", [NPOOL * 128, 32])
    cckv4 = cckv.rearrange("(n r) c -> n (r c)", r=4); ckpe4 = ckpe.rearrange("(n r) c -> n (r c)", r=4)
    sconv = din("sconv", [32, 512]); ptab = din("ptab", [1, 16 * NPG], I32)
    vec_in = {n: din(n, [1, 1024]) for n in ("lneg", "lneb", "ln1g", "ln1b", "ln2g", "ln2b")}
    qng = din("qng", [1, 384]); kvg = din("kvg", [1, 256]); convw = din("convw", [3, 512])
    w_in = din("w_in", [1024, D_IN]); w_uq = din("w_uq", [384, 768]); w_uk = din("w_uk", [256, 512])
    w_uv = din("w_uv", [256, 512]); w_om = din("w_om", [512, 1024]); w_co = din("w_co", [512, 1024])
    w_o = din("w_o", [1024, 1024]); w_up = din("w_up", [1024, 4096]); w_dn = din("w_dn", [4096, 1024])
    c_ident = din("c_ident", [128, 128]); c_tri = din("c_tri", [128, 128]); c_dmask = din("c_dmask", [128, 1024])
    c_rope_p = din("c_rope_p", [128, NBLK * 32]); c_rope_m = din("c_rope_m", [128, 32]); c_rope_s = din("c_rope_s", [128, 32])
    c_iota = din("c_iota", [128, 1])

    y_p = dout("y_p", [NSEQ, SEQ, 1024]); y_s = dout("y_s", [128, 1024])
    ckv_p = dout("ckv_p", [NSEQ, L, 256]); kpe_p = dout("kpe_p", [NSEQ, L, 32]); conv_p = dout("conv_p", [NSEQ * 2, 512])
    ckv_s = dout("ckv_s", [128, 256]); kpe_s = dout("kpe_s", [128, 32]); conv_s = dout("conv_s", [32, 512])

    NPIECE = 29
    wscr = nc.dram_tensor("wscr", [NPIECE, 128, 4096], BF16, kind="Internal").ap()

    dbg_outs = {}

    def dbg(name, ap, keys, shape):
        if name in debug:
            o = dout("dbg_" + name, shape)
            T.dma("pool", lambda e: e.dma_start(out=o, in_=ap), "dbg", r=keys)

    idf = sb("idf", [128, 128]); idb = sb("idb", [128, 128], BF16); tri = sb("tri", [128, 128], BF16)
    dmask = sb("dmask", [128, 1024], BF16)
    rope_p = sb("rope_p", [128, NBLK * 32]); rope_m = sb("rope_m", [128, 32]); rope_s = sb("rope_s", [128, 32])
    iota = sb("iota", [128, 1]); eps5 = sb("eps5", [128, 1]); eps6 = sb("eps6", [128, 1])
    vecs = {n: sb("v_" + n, [128, 1024]) for n in vec_in}
    gqB = sb("gqB", [128, 384]); gkvB = sb("gkvB", [128, 256]); cw = sb("cw", [128, 16])
    wuq = sb("wuq", [128, 3, 768], BF16); wuk = sb("wuk", [128, 2, 512], BF16); wuv = sb("wuv", [128, 2, 512], BF16)
    NSLOT = 3
    wslot = [sb(f"wslot{i}", [128, 4096], BF16) for i in range(NSLOT)]
    h32s = [sb("h32a", [128, 2, 1024]), sb("h32b", [128, 2, 1024])]; hTs = [sb("hTa", [128, 8, 256], BF16), sb("hTb", [128, 8, 256], BF16)]
    stats = sb("stats", [128, 12]); mv = sb("mv", [128, 2]); sd = sb("sd", [128, 1]); rs = sb("rs", [128, 1])
    ssq = sb("ssq", [128, 1]); junk = sb("junk", [128, 384], BF16)
    cqn = sb("cqn", [128, 1, 384], BF16); cqnT = sb("cqnT", [128, 3, 256], BF16)
    qs = sb("qs", [128, 768]); qr = sb("qr", [128, 1, 768], BF16); qT = sb("qT", [96, 8, 256], BF16)
    rt = sb("rt", [128, 4, 128])
    kvo = sb("kvo", [128, 2, 288]); kvb = sb("kvb", [128, 2, 290], BF16); ckvT = sb("ckvT", [128, 2, 256], BF16)
    kt = sb("kt", [128, 4, 16])
    cvi = sb("cvb", [32, 512]); cvo = cvi
    gb = sb("gb", [128, 4, 256], BF16); gcs = sb("gcs", [128, 4, 256], BF16); upad = sb("upad", [128, 4, 258])
    ct = sb("ct", [128, 256]); cg = sb("cg", [128, 4, 256], BF16)
    sgt = [sb(f"sgt{i}", [128, 2, 256], BF16) for i in range(2)]
    tm = [sb("tm0", [128, 2, 256], BF16)] * 2
    pTt = [sb(f"pTt{i}", [128, 256], BF16) for i in range(3)]
    rec = sb("rec", [128, 2]); otok = sb("otok", [128, 2, 512], BF16); oT = sb("oT", [128, 4, 256], BF16)
    aT = sb("aT", [128, 32, 256], BF16)
    hb = aT[:, 24:32, :].rearrange("p (b c) t -> p b (c t)", b=2)
    yc = aT[:, 0:8, :]; m1 = aT[:, 8:16, :]; sgm = aT[:, 16:24, :]
    YC = ["aT.0", "aT.1"]; M1 = ["aT.2", "aT.3"]; SGM = ["aT.4", "aT.5"]
    rl = [sb(f"rl{i}", [128, 2, 256], BF16) for i in range(2)]
    ulast = sb("ulast", [128, 32])
    ARENA = 12700
    arena = sb("arena", [128, ARENA])

    psf = [nc.alloc_psum_tensor(f"psf{i}", [128, 512], F32) for i in range(6)]
    psb = [nc.alloc_psum_tensor(f"psb{i}", [128, 1024], BF16) for i in range(2)]
    rf = Rot(range(6)); rb = Rot(range(2))

    class Carver:
        def __init__(self):
            self.off = 0

        def f32(self, n):
            a = arena[:, self.off:self.off + n]
            self.off += n
            assert self.off <= ARENA
            return a

        def bf(self, n):
            return self.f32((n + 1) // 2).bitcast(BF16)

        def i32(self, n):
            return self.f32(n).bitcast(I32)

    def ld(out_ap, in_ap, key, q="sp"):
        T.dma(q, lambda e: e.dma_start(out=out_ap, in_=in_ap), "l_" + key, w=[key])

    ld(idf[:], c_ident, "idf"); ld(rope_p[:], c_rope_p, "rope_p"); ld(rope_m[:], c_rope_m, "rope_m")
    ld(rope_s[:], c_rope_s, "rope_s"); ld(iota[:], c_iota, "iota")
    for n in vec_in:
        ld(vecs[n][:], vec_in[n].partition_broadcast(128), "v_" + n)
    ld(gqB[:], qng.partition_broadcast(128), "gqB"); ld(gkvB[:], kvg.partition_broadcast(128), "gkvB")
    T.op("pool", lambda e: e.memset(eps5[:], 1e-5), w=["eps5"])
    T.op("pool", lambda e: e.memset(eps6[:], 1e-6), w=["eps6"])
    T.op("pool", lambda e: e.memset(kvb[:], 1.0), w=["kvb.0", "kvb.1"])
    T.op("act", lambda e: e.copy(out=idb[:], in_=idf[:]), r=["idf"], w=["idb"])

    cs = Carver()
    stg = [cs.f32(4096) for _ in range(2)]
    stb = [cs.bf(4096) for _ in range(2)]
    ld(stg[0][:, 0:128], c_tri, "stg0")
    T.op("act", lambda e: e.copy(out=tri[:], in_=stg[0][:, 0:128]), r=["stg0"], w=["tri"])
    ld(stg[1][:, 0:1024], c_dmask, "stg1")
    T.op("dve", lambda e: e.tensor_copy(out=dmask[:], in_=stg[1][:, 0:1024]), r=["stg1"], w=["dmask"])
    ld(stg[0][:, 0:2304].rearrange("p (k c) -> p k c", k=3), w_uq.rearrange("(k p) c -> p k c", p=128), "stg0")
    T.op("act", lambda e: e.copy(out=wuq[:].rearrange("p k c -> p (k c)"), in_=stg[0][:, 0:2304]), r=["stg0"], w=["wuq"])
    ld(stg[1][:, 0:1024].rearrange("p (k c) -> p k c", k=2), w_uk.rearrange("(k p) c -> p k c", p=128), "stg1")
    T.op("dve", lambda e: e.tensor_copy(out=wuk[:].rearrange("p k c -> p (k c)"), in_=stg[1][:, 0:1024]), r=["stg1"], w=["wuk"])
    ld(stg[0][:, 0:1024].rearrange("p (k c) -> p k c", k=2), w_uv.rearrange("(k p) c -> p k c", p=128), "stg0")
    T.op("act", lambda e: e.copy(out=wuv[:].rearrange("p k c -> p (k c)"), in_=stg[0][:, 0:1024]), r=["stg0"], w=["wuv"])
    T.op("pool", lambda e: e.memset(cvi[:], 0.0), w=["cvb"])
    ld(cvi[0:3, :], convw, "cvb")
    pb = rf.next()
    for cc in range(4):
        T.op("pe", lambda e: e.transpose(out=psf[pb][:, cc * 4:cc * 4 + 4], in_=cvi[0:4, cc * 128:(cc + 1) * 128], identity=idf[0:4, 0:4]),
             r=["cvb", "idf"], w=[f"psf{pb}"])
    T.op("dve", lambda e: e.tensor_copy(out=cw[:], in_=psf[pb][:, 0:16]), r=[f"psf{pb}"], w=["cw"])

    pieces = {}
    for s_ in range(2):
        T.op("pool", lambda e: e.memset(eps5[:], 1e-5), w=["eps5", f"stg{s_}"] + [f"stg{s_}.{k}" for k in range(8)])

    def add_piece(name, src_ap, kc, ncol):
        idx = len(pieces)
        pieces[name] = (idx, kc, ncol)
        s = idx % 2
        n = kc * ncol
        sk = [f"stg{s}.{k}" for k in range(kc)]
        for k in range(kc):
            T.dma("sp", lambda e: e.dma_start(out=stg[s][:, k * ncol:(k + 1) * ncol], in_=src_ap[:, k, :]), f"l_stg{s}", w=[sk[k]])
        eng = "act" if idx % 2 == 0 else "dve"
        if eng == "act":
            T.op("act", lambda e: e.copy(out=stb[s][:, 0:n], in_=stg[s][:, 0:n]), r=sk, w=[f"stb{s}"])
        else:
            T.op("dve", lambda e: e.tensor_copy(out=stb[s][:, 0:n], in_=stg[s][:, 0:n]), r=sk, w=[f"stb{s}"])
        T.dma("pool", lambda e: e.dma_start(out=wscr[idx, :, 0:n], in_=stb[s][:, 0:n]), f"scrst{s}", r=[f"stb{s}"], w=[f"scr.{name}"])

    w_in_v = w_in.rearrange("(k p) c -> p k c", p=128)
    add_piece("A1", w_in_v[:, :, 0:384], 8, 384)
    add_piece("A2", w_in_v[:, :, 384:672], 8, 288)
    for i in range(7):
        add_piece(f"B{i}", w_in_v[:, :, 672 + 512 * i:672 + 512 * (i + 1)], 8, 512)
    w_up_v = w_up.rearrange("(k p) c -> p k c", p=128)
    w_dn_v = w_dn.rearrange("(k p) c -> p k c", p=128)
    for j in range(8):
        add_piece(f"U{j}", w_up_v[:, :, 512 * j:512 * (j + 1)], 8, 512)
        add_piece(f"D{j}", w_dn_v[:, 4 * j:4 * j + 4, :], 4, 1024)
    w_o_v = w_o.rearrange("(k p) c -> p k c", p=128)
    add_piece("O0", w_o_v[:, :, 0:512], 8, 512)
    add_piece("O1", w_o_v[:, :, 512:1024], 8, 512)
    add_piece("CO", w_co.rearrange("(k p) c -> p k c", p=128), 4, 1024)
    add_piece("OM", w_om.rearrange("(k p) c -> p k c", p=128), 4, 1024)
    assert len(pieces) == NPIECE

    wctr = [0]

    def getw(name):
        idx, kc, ncol = pieces[name]
        s = wctr[0] % NSLOT
        wctr[0] += 1
        n = kc * ncol
        T.dma("sp", lambda e: e.dma_start(out=wslot[s][:, 0:n], in_=wscr[idx, :, 0:n]), f"ws{s}", r=[f"scr.{name}"], w=[f"ws{s}"])
        return wslot[s][:, 0:n].rearrange("p (k c) -> p k c", k=kc), f"ws{s}"

    T.barrier()

    T.marks = []

    def chk(stage):
        T.marks.append((stage, dict(T.cnt)))
        if stop == stage:
            raise _Stop()

    cp = Carver()
    Kst = cp.bf(8 * L)[0:96, :].rearrange("p (h l) -> p h l", h=8)
    Vst = cp.bf(NCHV * 520).rearrange("p (c h d) -> p c h d", c=NCHV, h=8)
    T.op("pool", lambda e: e.memset(Vst[:, :, :, 64:65], 1.0), w=["Vst"])

    def mm(out, lhsT, rhs, start, stop, r, w, **kw):
        T.op("pe", lambda e: e.matmul(out, lhsT=lhsT, rhs=rhs, start=start, stop=stop, **kw), r=r, w=w)

    def tr(out, in_, R_in, r, w):
        T.op("pe", lambda e: e.transpose(out=out, in_=in_, identity=idb[0:R_in, 0:R_in]), r=list(r) + ["idb"], w=w)

    def layernorm(par, R, b, g, bt, want_hb=True):
        xb = h32s[par][0:R, b, :]
        k = f"h32.{par}.{b}"
        T.op("dve", lambda e: e.bn_stats(out=stats[0:R, 0:6], in_=xb[:, 0:512]), r=[k], w=["stats"])
        T.op("dve", lambda e: e.bn_stats(out=stats[0:R, 6:12], in_=xb[:, 512:1024]), r=[k], w=["stats"])
        T.op("dve", lambda e: e.bn_aggr(out=mv[0:R, :], in_=stats[0:R, :]), r=["stats"], w=["mv"])
        T.op("act", lambda e: e.activation(out=sd[0:R, :], in_=mv[0:R, 1:2], func=AF.Sqrt, bias=eps5[0:R, :], scale=1.0), r=["mv", "eps5"], w=["sd"])
        T.op("dve", lambda e: e.reciprocal(out=rs[0:R, :], in_=sd[0:R, :]), r=["sd"], w=["rs"])
        T.op("dve", lambda e: e.scalar_tensor_tensor(out=xb, in0=xb, scalar=mv[0:R, 0:1], in1=vecs[g][0:R, :], op0=ALU.subtract, op1=ALU.mult),
             r=[k, "mv", "v_" + g], w=[k])
        T.op("dve", lambda e: e.scalar_tensor_tensor(out=xb, in0=xb, scalar=rs[0:R, :], in1=vecs[bt][0:R, :], op0=ALU.mult, op1=ALU.add),
             r=[k, "rs", "v_" + bt], w=[k])
        if want_hb:
            T.op("act", lambda e: e.copy(out=hb[0:R, b, :], in_=xb), r=[k], w=[f"aT.{6 + b}"])

    def to_hT(par, R, b):
        bk = rb.next()
        for c in range(8):
            tr(psb[bk][:, c * 128:c * 128 + R], hb[0:R, b, c * 128:(c + 1) * 128], R, [f"aT.{6 + b}"], [f"psb{bk}"])
        T.op("dve", lambda e: e.tensor_copy(out=hTs[par][:, :, b * R:(b + 1) * R],
                                            in_=psb[bk][:, :].rearrange("p (c t) -> p c t", c=8)[:, :, 0:R]),
             r=[f"psb{bk}"], w=[f"hT.{par}.{b}"])

    def prologue(kind, seq, ti, par):
        R = 16 if kind == "meta" else 128
        nb = 2 if kind == "prompt" else 1
        H = h32s[par]
        hkeys = [f"h32.{par}.{b}" for b in range(nb)]
        if kind == "prompt":
            src = x_p[seq, ti * TT:(ti + 1) * TT, :].rearrange("(n p) d -> p n d", p=128)
            T.dma("sp", lambda e: e.dma_start(out=H[:, :, :], in_=src), f"x{par}", w=hkeys)
        elif kind == "meta":
            T.dma("sp", lambda e: e.dma_start(out=H[0:16, 0, :], in_=meta), f"x{par}", w=hkeys)
        else:
            T.dma("sp", lambda e: e.dma_start(out=H[:, 0, :], in_=x_s), f"x{par}", w=hkeys)
        for b in range(nb):
            layernorm(par, R, b, "lneg", "lneb")
            to_hT(par, R, b)

    def rms_scale(R, ps_ap, n, pkey):
        T.op("act", lambda e: e.activation(out=junk[0:R, 0:n], in_=ps_ap, func=AF.Square, accum_out=ssq[0:R, :]), r=[pkey], w=["junk", "ssq"])
        T.op("act", lambda e: e.activation(out=sd[0:R, :], in_=ssq[0:R, :], func=AF.Sqrt, bias=eps6[0:R, :], scale=1.0 / n), r=["ssq", "eps6"], w=["sd"])
        T.op("dve", lambda e: e.reciprocal(out=rs[0:R, :], in_=sd[0:R, :]), r=["sd"], w=["rs"])

    def rope_ops(R, x1, x2, o1, o2, cosb, sinb, t, rkeys, wkeys, tkey):
        T.op("dve", lambda e: e.tensor_tensor(out=t[0], in0=x1, in1=cosb, op=ALU.mult), r=rkeys, w=[tkey])
        T.op("dve", lambda e: e.tensor_tensor(out=t[1], in0=x2, in1=sinb, op=ALU.mult), r=rkeys, w=[tkey])
        T.op("dve", lambda e: e.tensor_tensor(out=t[2], in0=x2, in1=cosb, op=ALU.mult), r=rkeys, w=[tkey])
        T.op("dve", lambda e: e.tensor_tensor(out=t[3], in0=x1, in1=sinb, op=ALU.mult), r=rkeys, w=[tkey])
        T.op("dve", lambda e: e.tensor_tensor(out=o1, in0=t[0], in1=t[1], op=ALU.subtract), r=[tkey], w=wkeys)
        T.op("dve", lambda e: e.tensor_tensor(out=o2, in0=t[2], in1=t[3], op=ALU.add), r=[tkey], w=wkeys)

    yst_ctr = [0]

    def tile(kind, seq, ti, par, nxt, first):
        is_meta = kind == "meta"
        is_samp = kind == "sample"
        R = 16 if is_meta else 128
        nb = 2 if kind == "prompt" else 1
        Tt = nb * R
        hkeys = [f"h32.{par}.{b}" for b in range(nb)]
        hTk = [f"hT.{par}.{b}" for b in range(nb)]
        H = h32s[par]; HT = hTs[par]
        if first:
            prologue(kind, seq, ti, par)
        if kind == "prompt" and seq == 0 and ti == 0:
            dbg("h0", H[:, 0, :], [f"h32.{par}.0"], [128, 1024])

        def ropetab(b):
            if is_meta:
                return rope_m[0:R, :]
            if is_samp:
                return rope_s[0:R, :]
            j = ti * 2 + b
            return rope_p[0:R, j * 32:(j + 1) * 32]

        def chain():
            if not is_meta:
                for b in range(nb):
                    A1, k1 = getw("A1")
                    p1 = rC.next()
                    for kc in range(8):
                        mm(psf[p1][0:R, 0:384], HT[:, kc, b * R:(b + 1) * R], A1[:, kc, :], kc == 0, kc == 7, [hTk[b], k1], [f"psf{p1}"])
                    yield
                    rms_scale(R, psf[p1][0:R, 0:384], 384, f"psf{p1}")
                    T.op("dve", lambda e: e.scalar_tensor_tensor(out=cqn[0:R, 0, :], in0=psf[p1][0:R, 0:384], scalar=rs[0:R, :], in1=gqB[0:R, :],
                                                                 op0=ALU.mult, op1=ALU.mult), r=[f"psf{p1}", "rs", "gqB"], w=["cqn"])
                    yield
                    bk = rb.next()
                    for c in range(3):
                        tr(psb[bk][:, c * 128:c * 128 + R], cqn[0:R, 0, c * 128:(c + 1) * 128], R, ["cqn"], [f"psb{bk}"])
                    T.op("act", lambda e: e.copy(out=cqnT[:, :, b * R:(b + 1) * R], in_=psb[bk][:, 0:384].rearrange("p (c t) -> p c t", c=3)[:, :, 0:R]),
                         r=[f"psb{bk}"], w=[f"cqnT.{b}"])
                    yield
                    p2 = rC.next(); p3 = rC.next()
                    for kc in range(3):
                        mm(psf[p2][0:R, 0:512], cqnT[:, kc, b * R:(b + 1) * R], wuq[:, kc, 0:512], kc == 0, kc == 2, [f"cqnT.{b}", "wuq"], [f"psf{p2}"])
                    for kc in range(3):
                        mm(psf[p3][0:R, 0:256], cqnT[:, kc, b * R:(b + 1) * R], wuq[:, kc, 512:768], kc == 0, kc == 2, [f"cqnT.{b}", "wuq"], [f"psf{p3}"])
                    yield
                    T.op("act", lambda e: e.copy(out=qs[0:R, 0:512], in_=psf[p2][0:R, 0:512]), r=[f"psf{p2}"], w=["qs"])
                    T.op("dve", lambda e: e.tensor_copy(out=qs[0:R, 512:768], in_=psf[p3][0:R, 0:256]), r=[f"psf{p3}"], w=["qs"])
                    q3 = qs[0:R, :].rearrange("p (h c) -> p h c", h=8)
                    qr3 = qr[0:R, 0, :].rearrange("p (h c) -> p h c", h=8)
                    tab = ropetab(b)
                    cosb = tab[:, 0:16].unsqueeze(1).to_broadcast([R, 8, 16])
                    sinb = tab[:, 16:32].unsqueeze(1).to_broadcast([R, 8, 16])
                    T.op("act", lambda e: e.copy(out=qr3[:, :, 0:64], in_=q3[:, :, 0:64]), r=["qs"], w=["qr"])
                    tt = [rt[0:R, i, :].rearrange("p (h c) -> p h c", h=8) for i in range(4)]
                    rope_ops(R, q3[:, :, 64:80], q3[:, :, 80:96], qr3[:, :, 64:80], qr3[:, :, 80:96], cosb, sinb, tt, ["qs", "rope_p", "rope_s"], ["qr"], "rt")
                    yield
                    bk = rb.next()
                    for h in range(8):
                        tr(psb[bk][0:96, h * 128:h * 128 + R], qr[0:R, 0, h * 96:(h + 1) * 96], R, ["qr"], [f"psb{bk}"])
                    T.op("act", lambda e: e.copy(out=qT[:, :, b * R:(b + 1) * R], in_=psb[bk][0:96, :].rearrange("p (h t) -> p h t", h=8)[:, :, 0:R]),
                         r=[f"psb{bk}"], w=[f"qT.{b}"])

            for b in range(nb):
                A2, k2 = getw("A2")
                pa = rC.next()
                for kc in range(8):
                    mm(psf[pa][0:R, 0:288], HT[:, kc, b * R:(b + 1) * R], A2[:, kc, :], kc == 0, kc == 7, [hTk[b], k2], [f"psf{pa}"])
                yield
                rms_scale(R, psf[pa][0:R, 0:256], 256, f"psf{pa}")
                T.op("dve", lambda e: e.scalar_tensor_tensor(out=kvo[0:R, b, 0:256], in0=psf[pa][0:R, 0:256], scalar=rs[0:R, :], in1=gkvB[0:R, :],
                                                             op0=ALU.mult, op1=ALU.mult), r=[f"psf{pa}", "rs", "gkvB"], w=[f"kvo.{b}"])
                tab = ropetab(b)
                tt = [kt[0:R, i, :] for i in range(4)]
                rope_ops(R, psf[pa][0:R, 256:272], psf[pa][0:R, 272:288], kvo[0:R, b, 256:272], kvo[0:R, b, 272:288],
                         tab[:, 0:16], tab[:, 16:32], tt, [f"psf{pa}", "rope_p", "rope_m", "rope_s"], [f"kvo.{b}"], "kt")
                if is_meta:
                    o1, o2 = ckv_p[seq, 0:16, :], kpe_p[seq, 0:16, :]
                elif is_samp:
                    o1, o2 = ckv_s, kpe_s
                else:
                    r0 = 16 + ti * TT + b * 128
                    o1, o2 = ckv_p[seq, r0:r0 + 128, :], kpe_p[seq, r0:r0 + 128, :]
                yield
                T.dma("pool", lambda e: e.dma_start(out=o1, in_=kvo[0:R, b, 0:256]), f"kvst{b}", r=[f"kvo.{b}"])
                T.dma("pool", lambda e: e.dma_start(out=o2, in_=kvo[0:R, b, 256:288]), f"kvst{b}", r=[f"kvo.{b}"])
                T.op("act", lambda e: e.copy(out=kvb[0:R, b, 0:288], in_=kvo[0:R, b, :]), r=[f"kvo.{b}"], w=[f"kvb.{b}"])
                yield
                bk = rb.next()
                for c in range(2):
                    tr(psb[bk][:, c * 128:c * 128 + R], kvb[0:R, b, c * 128:(c + 1) * 128], R, [f"kvb.{b}"], [f"psb{bk}"])
                tr(psb[bk][0:96, 256:256 + R], kvb[0:R, b, 192:288], R, [f"kvb.{b}"], [f"psb{bk}"])
                T.op("dve", lambda e: e.tensor_copy(out=ckvT[:, :, b * R:(b + 1) * R], in_=psb[bk][:, 0:256].rearrange("p (c t) -> p c t", c=2)[:, :, 0:R]),
                     r=[f"psb{bk}"], w=[f"ckvT.{b}"])
                if is_samp:
                    T.op("act", lambda e: e.copy(out=kpTn[64:96, 0:128], in_=psb[bk][64:96, 256:384]), r=[f"psb{bk}"], w=["kpTn"])
                else:
                    kc0 = 0 if is_meta else 16 + ti * TT + b * 128
                    T.op("act", lambda e: e.copy(out=Kst[64:96, :, kc0:kc0 + R], in_=psb[bk][64:96, 256:256 + R].unsqueeze(1).to_broadcast([32, 8, R])),
                         r=[f"psb{bk}"], w=[f"K.{0 if is_meta else 1 + ti}"])
            if not is_samp:
                kkey = f"K.{0 if is_meta else 1 + ti}"
                kc0 = 0 if is_meta else 16 + ti * TT
                ckk = [f"ckvT.{b}" for b in range(nb)]
                for h in range(8):
                    yield
                    pk = rC.next()
                    for c in range(2):
                        mm(psf[pk][0:64, 0:Tt], wuk[:, c, h * 64:(h + 1) * 64], ckvT[:, c, 0:Tt], c == 0, c == 1, ckk + ["wuk"], [f"psf{pk}"])
                    if h % 2 == 0:
                        T.op("act", lambda e: e.copy(out=Kst[0:64, h, kc0:kc0 + Tt], in_=psf[pk][0:64, 0:Tt]), r=[f"psf{pk}"], w=[kkey])
                    else:
                        T.op("dve", lambda e: e.tensor_copy(out=Kst[0:64, h, kc0:kc0 + Tt], in_=psf[pk][0:64, 0:Tt]), r=[f"psf{pk}"], w=[kkey])
                for b in range(nb):
                    yield
                    pv = rC.next()
                    vch = 0 if is_meta else 1 + ti * 2 + b
                    for c in range(2):
                        mm(psf[pv][0:R, 0:512], ckvT[:, c, b * R:(b + 1) * R], wuv[:, c, :], c == 0, c == 1, [f"ckvT.{b}", "wuv"], [f"psf{pv}"])
                    T.op("dve", lambda e: e.tensor_copy(out=Vst[0:R, vch, :, 0:64], in_=psf[pv][0:R, 0:512].rearrange("p (h d) -> p h d", h=8)),
                         r=[f"psf{pv}"], w=["Vst"])

            yield

        def bulk():
            def bpair(Bp, kB, j0):
                pbk = rBk.next()
                for j2 in range(2):
                    j = j0 + j2
                    for kc in range(8):
                        mm(psf[pbk][:, j2 * 256:j2 * 256 + Tt], Bp[:, kc, j * 128:(j + 1) * 128], HT[:, kc, 0:Tt], kc == 0, kc == 7, hTk + [kB], [f"psf{pbk}"])
                return pbk, psf[pbk][:, :].rearrange("p (a t) -> p a t", a=2)[:, :, 0:Tt]

            if not is_meta:
                yield
                Bp, kB = getw("B0")
                for j0 in (0, 2):
                    yield
                    pbk, pv2 = bpair(Bp, kB, j0)
                    T.op("act", lambda e: e.copy(out=gb[:, j0:j0 + 2, 0:Tt], in_=pv2), r=[f"psf{pbk}"], w=["gb"])
            yield
            Bp, kB = getw("B1")
            for j0 in (0, 2):
                yield
                pbk, pv2 = bpair(Bp, kB, j0)
                T.op("act", lambda e: e.copy(out=gcs[:, j0:j0 + 2, 0:Tt], in_=pv2), r=[f"psf{pbk}"], w=["gcs"])
            yield
            Bp, kB = getw("B2")
            if is_samp:
                T.dma("sp", lambda e: e.dma_start(out=cvi[:, :], in_=sconv), "l_cvi", w=["cvb"])
                pc = rBk.next()
                for cc in range(4):
                    T.op("pe", lambda e: e.transpose(out=psf[pc][:, cc * 32:(cc + 1) * 32], in_=cvi[0:32, cc * 128:(cc + 1) * 128], identity=idf[0:32, 0:32]),
                         r=["cvb", "idf"], w=[f"psf{pc}"])
                for cc in range(4):
                    uv = upad[:, cc, 0:160].rearrange("p (b s) -> p b s", s=10)
                    T.op("dve", lambda e: e.tensor_copy(out=uv[:, :, 0:2], in_=psf[pc][:, cc * 32:(cc + 1) * 32].rearrange("p (b s) -> p b s", s=2)),
                         r=[f"psf{pc}"], w=[f"upad.{cc}"])
            for j0 in (0, 2):
                yield
                pbk, pv2 = bpair(Bp, kB, j0)
                for j2 in range(2):
                    cc = j0 + j2
                    uk = f"upad.{cc}"
                    if is_samp:
                        uv = upad[:, cc, 0:160].rearrange("p (b s) -> p b s", s=10)
                        T.op("dve", lambda e: e.tensor_tensor(out=uv[:, :, 2:10], in0=psf[pbk][:, j2 * 256:j2 * 256 + 128].rearrange("p (b s) -> p b s", s=8),
                                                              in1=gcs[:, cc, 0:128].rearrange("p (b s) -> p b s", s=8), op=ALU.mult),
                             r=[f"psf{pbk}", "gcs"], w=[uk])
                        U = [uv[:, :, k:k + 8] for k in range(3)]
                        ctv = ct[:, 0:128].rearrange("p (b s) -> p b s", s=8)
                        cgv = cg[:, cc, 0:128].rearrange("p (b s) -> p b s", s=8)
                        gbv = gb[:, cc, 0:128].rearrange("p (b s) -> p b s", s=8)
                    else:
                        T.op("dve", lambda e: e.tensor_tensor(out=upad[:, cc, 2:2 + Tt], in0=psf[pbk][:, j2 * 256:j2 * 256 + Tt], in1=gcs[:, cc, 0:Tt], op=ALU.mult),
                             r=[f"psf{pbk}", "gcs"], w=[uk])
                        U = [upad[:, cc, k:k + Tt] for k in range(3)]
                        ctv = ct[:, 0:Tt]; cgv = cg[:, cc, 0:Tt]; gbv = gb[:, cc, 0:Tt]
                    if not is_meta:
                        T.op("dve", lambda e: e.tensor_scalar(out=ctv, in0=U[0], scalar1=cw[:, cc * 4:cc * 4 + 1], scalar2=None, op0=ALU.mult), r=[uk, "cw"], w=["ct"])
                        T.op("dve", lambda e: e.scalar_tensor_tensor(out=ctv, in0=U[1], scalar=cw[:, cc * 4 + 1:cc * 4 + 2], in1=ctv, op0=ALU.mult, op1=ALU.add),
                             r=[uk, "cw", "ct"], w=["ct"])
                        T.op("dve", lambda e: e.scalar_tensor_tensor(out=ctv, in0=U[2], scalar=cw[:, cc * 4 + 2:cc * 4 + 3], in1=ctv, op0=ALU.mult, op1=ALU.add),
                             r=[uk, "cw", "ct"], w=["ct"])
                        T.op("pool", lambda e: e.tensor_tensor(out=cgv, in0=ctv, in1=gbv, op=ALU.mult), r=["ct", "gb"], w=["cg"])
                    if is_samp:
                        T.op("dve", lambda e: e.tensor_copy(out=ulast[:, :].rearrange("p (b s) -> p b s", s=2), in_=uv[:, :, 8:10]), r=[uk], w=["ulast"])
                        pcc = rBk.next()
                        T.op("pe", lambda e: e.transpose(out=psf[pcc][0:32, 0:128], in_=ulast[:, :], identity=idf[:, :]), r=["ulast", "idf"], w=[f"psf{pcc}"])
                        T.op("act", lambda e: e.copy(out=cvo[0:32, cc * 128:(cc + 1) * 128], in_=psf[pcc][0:32, 0:128]), r=[f"psf{pcc}"], w=["cvb"])
                    else:
                        T.op("dve", lambda e: e.tensor_copy(out=upad[:, cc, 0:2], in_=upad[:, cc, Tt:Tt + 2]), r=[uk], w=[uk])
                        if kind == "prompt" and ti == NT - 1:
                            pcc = rBk.next()
                            T.op("pe", lambda e: e.transpose(out=psf[pcc][0:2, 0:128], in_=upad[:, cc, 0:2], identity=idf[:, :]), r=[uk, "idf"], w=[f"psf{pcc}"])
                            T.op("act", lambda e: e.copy(out=cvo[0:2, cc * 128:(cc + 1) * 128], in_=psf[pcc][0:2, 0:128]), r=[f"psf{pcc}"], w=["cvb"])
            if is_samp:
                T.dma("pool", lambda e: e.dma_start(out=conv_s, in_=cvo[0:32, :]), "cvst", r=["cvb"])
            elif kind == "prompt" and ti == NT - 1:
                T.dma("pool", lambda e: e.dma_start(out=conv_p[2 * seq:2 * seq + 2, :], in_=cvo[0:2, :]), "cvst", r=["cvb"])
            if is_meta:
                return

            yield
            CO, kco = getw("CO")
            for oc in (0, 2, 4, 6):
                yield
                pbk = rBk.next()
                for j2 in range(2):
                    for kc in range(4):
                        mm(psf[pbk][:, j2 * 256:j2 * 256 + Tt], CO[:, kc, (oc + j2) * 128:(oc + j2 + 1) * 128], cg[:, kc, 0:Tt], kc == 0, kc == 3, ["cg", kco], [f"psf{pbk}"])
                pv2 = psf[pbk][:, :].rearrange("p (a t) -> p a t", a=2)[:, :, 0:Tt]
                T.op("dve", lambda e: e.tensor_copy(out=yc[:, oc:oc + 2, 0:Tt], in_=pv2), r=[f"psf{pbk}"], w=YC)
            si = 0
            for pi in (3, 4):
                yield
                Bp, kB = getw(f"B{pi}")
                for j0 in (0, 2):
                    oc = (pi - 3) * 4 + j0
                    yield
                    pbk, pv2 = bpair(Bp, kB, j0)
                    sg = sgt[si % 2]; sk = f"sgt{si % 2}"; si += 1
                    T.op("act", lambda e: e.activation(out=sg[:, :, 0:Tt], in_=pv2, func=AF.Sigmoid), r=[f"psf{pbk}"], w=[sk])
                    T.op("pool", lambda e: e.tensor_tensor(out=m1[:, oc:oc + 2, 0:Tt], in0=sg[:, :, 0:Tt], in1=yc[:, oc:oc + 2, 0:Tt], op=ALU.mult), r=[sk] + YC, w=M1)
            for pi in (5, 6):
                yield
                Bp, kB = getw(f"B{pi}")
                for j0 in (0, 2):
                    oc = (pi - 5) * 4 + j0
                    yield
                    pbk, pv2 = bpair(Bp, kB, j0)
                    T.op("act", lambda e: e.activation(out=sgm[:, oc:oc + 2, 0:Tt], in_=pv2, func=AF.Sigmoid), r=[f"psf{pbk}"], w=SGM)

            yield

        rC = Rot([0, 1, 2]); rBk = Rot([3, 4, 5])
        gc, gb_ = chain(), bulk()
        alive = [True, True]
        while alive[0] or alive[1]:
            if alive[1]:
                try:
                    next(gb_)
                except StopIteration:
                    alive[1] = False
            if alive[0]:
                for _ in range(2):
                    try:
                        next(gc)
                    except StopIteration:
                        alive[0] = False
                        break
        if is_meta:
            if nxt is not None:
                prologue(nxt[0], nxt[1], nxt[2], 1 - par)
            return

        chk("front")
        if kind == "prompt":
            rS = Rot([0, 1, 2, 3])
            qk = ["qT.0", "qT.1"]
            LA = 2
            items = []
            for h in range(8):
                chunks = [(0, 16, 0, None, "K.0")]
                for c in range(2 * ti):
                    chunks.append((16 + 128 * c, 128, 1 + c, None, f"K.{1 + c // 2}"))
                for jj in range(2):
                    c = 2 * ti + jj
                    chunks.append((16 + 128 * c, 128, 1 + c, jj, f"K.{1 + ti}"))
                for ci, ch in enumerate(chunks):
                    items.append((h, ci == 0, ci == len(chunks) - 1) + ch)
            st = {}

            def emitS(i):
                h, isfirst, islast, k0, nk, vch, jj, kkey = items[i]
                q0 = 128 * jj if jj is not None else 0
                sbk = rS.next()
                mm(psf[sbk][0:nk, q0:Tt], Kst[:, h, k0:k0 + nk], qT[:, h, q0:Tt], True, True, [kkey] + qk, [f"psf{sbk}"])
                pt_ = pTt[i % 3]; pk_ = f"pTt{i % 3}"
                T.op("act", lambda e: e.activation(out=pt_[0:nk, q0:Tt], in_=psf[sbk][0:nk, q0:Tt], func=AF.Exp, scale=SM_SCALE), r=[f"psf{sbk}"], w=[pk_])
                if jj is not None:
                    T.op("pool", lambda e: e.tensor_tensor(out=pt_[:, q0:q0 + 128], in0=pt_[:, q0:q0 + 128], in1=tri[:, :], op=ALU.mult), r=[pk_, "tri"], w=[pk_])

            def emitPV(i):
                h, isfirst, islast, k0, nk, vch, jj, kkey = items[i]
                ob = 4 + h % 2
                Ov = psf[ob][:, 0:130].rearrange("p (a d) -> p a d", a=2)
                pt_ = pTt[i % 3]; pk_ = f"pTt{i % 3}"
                first = isfirst
                for qb in range(jj if jj is not None else 0, 2):
                    last = (jj == qb)
                    mm(Ov[:, qb, :], pt_[0:nk, qb * 128:(qb + 1) * 128], Vst[0:nk, vch, h, :], first, last, [pk_, "Vst"], [f"psf{ob}"], skip_group_check=True)
                    first = False
                if islast:
                    T.op("dve", lambda e: e.reciprocal(out=rec[:, :], in_=Ov[:, :, 64]), r=[f"psf{ob}"], w=["rec"])
                    for qb in range(2):
                        T.op("dve", lambda e: e.tensor_scalar(out=otok[:, qb, h * 64:(h + 1) * 64], in0=Ov[:, qb, 0:64], scalar1=rec[:, qb:qb + 1], scalar2=None, op0=ALU.mult),
                             r=[f"psf{ob}", "rec"], w=[f"otok.{qb}"])

            n_it = len(items)
            for i in range(n_it + LA):
                if i < n_it:
                    emitS(i)
                if i - LA >= 0:
                    emitPV(i - LA)
        else:
            decode_attention()

        chk("attn")
        for b in range(nb):
            bk = rb.next()
            for c in range(4):
                tr(psb[bk][:, c * 128:c * 128 + R], otok[0:R, b, c * 128:(c + 1) * 128], R, [f"otok.{b}"], [f"psb{bk}"])
            T.op("dve", lambda e: e.tensor_copy(out=oT[:, :, b * R:(b + 1) * R], in_=psb[bk][:, 0:512].rearrange("p (c t) -> p c t", c=4)[:, :, 0:R]),
                 r=[f"psb{bk}"], w=[f"oT.{b}"])
        oTk = [f"oT.{b}" for b in range(nb)]
        OM, kom = getw("OM")
        ti_ = 0
        for oc in (0, 2, 4, 6):
            pbk = rf.next()
            for j2 in range(2):
                for kc in range(4):
                    mm(psf[pbk][:, j2 * 256:j2 * 256 + Tt], OM[:, kc, (oc + j2) * 128:(oc + j2 + 1) * 128], oT[:, kc, 0:Tt], kc == 0, kc == 3, oTk + [kom], [f"psf{pbk}"])
            pv2 = psf[pbk][:, :].rearrange("p (a t) -> p a t", a=2)[:, :, 0:Tt]
            tmv = tm[ti_ % 2]; tk = "tm0"; ti_ += 1
            T.op("dve", lambda e: e.tensor_tensor(out=tmv[:, :, 0:Tt], in0=pv2, in1=sgm[:, oc:oc + 2, 0:Tt], op=ALU.mult), r=[f"psf{pbk}"] + SGM, w=[tk])
            T.op("dve", lambda e: e.tensor_tensor(out=m1[:, oc:oc + 2, 0:Tt], in0=tmv[:, :, 0:Tt], in1=m1[:, oc:oc + 2, 0:Tt], op=ALU.add), r=[tk] + M1, w=M1)
        Ow = [getw("O0"), getw("O1")]
        for b in range(nb):
            for half in range(2):
                Oh, koh = Ow[half]
                pm = rf.next()
                for kc in range(8):
                    mm(psf[pm][0:R, 0:512], m1[:, kc, b * R:(b + 1) * R], Oh[:, kc, :], kc == 0, kc == 7, M1 + [koh], [f"psf{pm}"])
                hv = H[0:R, b, half * 512:(half + 1) * 512]
                T.op("dve", lambda e: e.scalar_tensor_tensor(out=hv, in0=hv, scalar=ALPHA, in1=psf[pm][0:R, 0:512], op0=ALU.mult, op1=ALU.add),
                     r=[f"h32.{par}.{b}", f"psf{pm}"], w=[f"h32.{par}.{b}"])
            layernorm(par, R, b, "ln1g", "ln1b")
            to_hT(par, R, b)
        chk("post")
        rU = Rot([4, 5])
        ri = [0]

        def up(j):
            U, ku = getw(f"U{j}")
            for p in range(2):
                pu = rU.next()
                for j2 in range(2):
                    c = 2 * p + j2
                    for kc in range(8):
                        mm(psf[pu][:, j2 * 256:j2 * 256 + Tt], U[:, kc, c * 128:(c + 1) * 128], HT[:, kc, 0:Tt], kc == 0, kc == 7, hTk + [ku], [f"psf{pu}"])
                pv2 = psf[pu][:, :].rearrange("p (a t) -> p a t", a=2)[:, :, 0:Tt]
                rv = rl[ri[0] % 2]; rk = f"rl{ri[0] % 2}"; ri[0] += 1
                T.op("act", lambda e: e.activation(out=rv[:, :, 0:Tt], in_=pv2, func=AF.Relu), r=[f"psf{pu}"], w=[rk])
                fc0 = 4 * j + 2 * p
                T.op("pool", lambda e: e.tensor_tensor(out=aT[:, fc0:fc0 + 2, 0:Tt], in0=rv[:, :, 0:Tt], in1=rv[:, :, 0:Tt], op=ALU.mult), r=[rk], w=[f"aT.{j}"])

        def down(j):
            D, kd = getw(f"D{j}")
            for f in range(4):
                fc = 4 * j + f
                for b in range(nb):
                    for half in range(2):
                        pa = b * 2 + half
                        mm(psf[pa][0:R, 0:512], aT[:, fc, b * R:(b + 1) * R], D[:, f, half * 512:(half + 1) * 512], fc == 0, fc == 31, [f"aT.{j}", kd], [f"psf{pa}"])

        up(0)
        for j in range(8):
            if j + 1 < 8:
                up(j + 1)
            down(j)
            if j == 1 and nxt is not None:
                prologue(nxt[0], nxt[1], nxt[2], 1 - par)
        for b in range(nb):
            for half in range(2):
                pa = b * 2 + half
                hv = H[0:R, b, half * 512:(half + 1) * 512]
                T.op("dve", lambda e: e.scalar_tensor_tensor(out=hv, in0=hv, scalar=ALPHA, in1=psf[pa][0:R, 0:512], op0=ALU.mult, op1=ALU.add),
                     r=[f"h32.{par}.{b}", f"psf{pa}"], w=[f"h32.{par}.{b}"])
            layernorm(par, R, b, "ln2g", "ln2b", want_hb=False)
            if is_samp:
                dst = y_s
            else:
                r0 = ti * TT + b * 128
                dst = y_p[seq, r0:r0 + 128, :]
            T.dma("pool", lambda e: e.dma_start(out=dst, in_=H[0:R, b, :]), f"yst{b}", r=[f"h32.{par}.{b}"])

    dec = {}

    def decode_attention():
        pgc32, pgk32, pgb, pgT, kpT, Pt = dec["pgc32"], dec["pgk32"], dec["pgb"], dec["pgT"], dec["kpT"], dec["Pt"]
        qlat, olT, idx, oln, Ps = dec["qlat"], dec["olT"], dec["idx"], dec["oln"], dec["Ps"]
        for h in range(8):
            for cc in range(2):
                pq = rf.next()
                mm(psf[pq][:, 0:128], wukT[0:64, h, cc * 128:(cc + 1) * 128], qT[0:64, h, 0:128], True, True, ["wukT", "qT.0"], [f"psf{pq}"])
                dstv = qlat[:, cc * 1024:(cc + 1) * 1024].rearrange("p (b x) -> p b x", b=16)[:, :, h * 8:(h + 1) * 8]
                srcv = psf[pq][:, 0:128].rearrange("p (b s) -> p b s", s=8)
                if (h + cc) % 2 == 0:
                    T.op("act", lambda e: e.copy(out=dstv, in_=srcv), r=[f"psf{pq}"], w=["qlat"])
                else:
                    T.op("dve", lambda e: e.tensor_copy(out=dstv, in_=srcv), r=[f"psf{pq}"], w=["qlat"])
        rS = Rot([0, 1, 2, 3])
        NG = NPG // 4
        items = []
        for b in range(16):
            for g in range(NG):
                items.append(("grp", b, g))
            items.append(("self", b, 0))
        slot_of = {}
        sl = 0
        slot3 = {}
        for i, it in enumerate(items):
            if it[0] == "grp":
                slot_of[i] = sl % 2
                slot3[i] = sl % 3
                sl += 1
        sbank = {}

        def qviews(b):
            ql = [qlat[:, cc * 1024 + b * 64:cc * 1024 + (b + 1) * 64] for cc in range(2)]
            qpe = qT[0:96, :, b * 8:(b + 1) * 8]
            return ql, qpe

        def stG(i):
            kind_, b, g = items[i]
            if kind_ != "grp":
                return
            s = slot3[i]
            col = b * NG + g
            T.dma("pool", lambda e: e.indirect_dma_start(out=pgc32[s][:, :], out_offset=None, in_=cckv4,
                                                          in_offset=bass.IndirectOffsetOnAxis(ap=idx[:, col:col + 1], axis=0)),
                  f"pg{s}a", r=["idx"], w=[f"pg32.{s}.a"])
            T.dma("pool", lambda e: e.indirect_dma_start(out=pgk32[s][:, :], out_offset=None, in_=ckpe4,
                                                          in_offset=bass.IndirectOffsetOnAxis(ap=idx[:, col:col + 1], axis=0)),
                  f"pg{s}b", r=["idx"], w=[f"pg32.{s}.b"])

        def stA1(i):
            kind_, b, g = items[i]
            if kind_ != "grp":
                return
            s3 = slot3[i]
            T.op("dve", lambda e: e.tensor_copy(out=pgb[s3][:, :, 0:256], in_=pgc32[s3][:, :].rearrange("p (r c) -> p r c", r=4)), r=[f"pg32.{s3}.a"], w=[f"pgb.{s3}"])
            T.op("act", lambda e: e.copy(out=pgb[s3][:, :, 256:288], in_=pgk32[s3][:, :].rearrange("p (r c) -> p r c", r=4)), r=[f"pg32.{s3}.b"], w=[f"pgb.{s3}"])
            for j in range(4):
                for cc in range(2):
                    tr(psb[0][:, (j * 2 + cc) * 128:(j * 2 + cc + 1) * 128], pgb[s3][:, j, cc * 128:(cc + 1) * 128], 128, [f"pgb.{s3}"], ["psb0"])
                tr(psb[1][0:96, j * 128:(j + 1) * 128], pgb[s3][:, j, 192:288], 128, [f"pgb.{s3}"], ["psb1"])
            s = slot_of[i]
            T.op("dve", lambda e: e.tensor_copy(out=pgT[s][:, :], in_=psb[0][:, :]), r=["psb0"], w=[f"pgT.{s}"])
            T.op("act", lambda e: e.copy(out=kpT[s][64:96, :], in_=psb[1][64:96, 0:512]), r=["psb1"], w=[f"kpT.{s}"])

        def stA2(i):
            kind_, b, g = items[i]
            ql, qpe = qviews(b)
            sbk = rS.next()
            sbank[i] = sbk
            if kind_ == "grp":
                s = slot_of[i]
                for j in range(4):
                    so = psf[sbk][:, j * 64:(j + 1) * 64]
                    mm(so, pgT[s][:, (j * 2) * 128:(j * 2 + 1) * 128], ql[0], True, False, [f"pgT.{s}", "qlat"], [f"psf{sbk}"])
                    mm(so, pgT[s][:, (j * 2 + 1) * 128:(j * 2 + 2) * 128], ql[1], False, False, [f"pgT.{s}", "qlat"], [f"psf{sbk}"])
                    mm(so, kpT[s][0:96, j * 128:(j + 1) * 128], qpe, False, True, [f"kpT.{s}", "qT.0"], [f"psf{sbk}"])
                T.op("act", lambda e: e.activation(out=Pt[s][:, :], in_=psf[sbk][:, 0:256], func=AF.Exp, scale=SM_SCALE), r=[f"psf{sbk}"], w=[f"Pt.{s}"])
            else:
                so = psf[sbk][:, 0:64]
                mm(so, ckvT[:, 0, 0:128], ql[0], True, False, ["ckvT.0", "qlat"], [f"psf{sbk}"])
                mm(so, ckvT[:, 1, 0:128], ql[1], False, False, ["ckvT.0", "qlat"], [f"psf{sbk}"])
                mm(so, kpTn[0:96, 0:128], qpe, False, True, ["kpTn", "qT.0"], [f"psf{sbk}"])
                T.op("act", lambda e: e.activation(out=Ps[:, :], in_=so, func=AF.Exp, scale=SM_SCALE), r=[f"psf{sbk}"], w=["Ps"])
                T.op("pool", lambda e: e.tensor_tensor(out=Ps[:, :], in0=Ps[:, :], in1=dmask[:, b * 64:(b + 1) * 64], op=ALU.mult), r=["Ps", "dmask"], w=["Ps"])

        def stB(i):
            kind_, b, g = items[i]
            ob = 4 + b % 2
            O = psf[ob]
            if kind_ == "grp":
                s = slot_of[i]
                s3 = slot3[i]
                for j in range(4):
                    mm(O[0:64, 0:289], Pt[s][:, j * 64:(j + 1) * 64], pgb[s3][:, j, 0:289], g == 0 and j == 0, False, [f"Pt.{s}", f"pgb.{s3}"], [f"psf{ob}"])
                return
            mm(O[0:64, 0:289], Ps[:, :], kvb[:, 0, 0:289], NG == 0, True, ["Ps", "kvb.0"], [f"psf{ob}"])
            T.op("dve", lambda e: e.reciprocal(out=rec[0:64, 0:1], in_=O[0:64, 288:289]), r=[f"psf{ob}"], w=["rec"])
            T.op("dve", lambda e: e.tensor_scalar(out=oln[0:64, :], in0=O[0:64, 0:256], scalar1=rec[0:64, 0:1], scalar2=None, op0=ALU.mult), r=[f"psf{ob}", "rec"], w=["oln"])
            bk = rb.next()
            for cc in range(2):
                tr(psb[bk][:, cc * 64:(cc + 1) * 64], oln[0:64, cc * 128:(cc + 1) * 128], 64, ["oln"], [f"psb{bk}"])
            for cc in range(2):
                dstv = olT[:, cc * 1024:(cc + 1) * 1024].rearrange("p (h t) -> p h t", h=8)[:, :, b * 8:(b + 1) * 8]
                srcv = psb[bk][:, cc * 64:(cc + 1) * 64].rearrange("p (h s) -> p h s", h=8)
                if cc == 0:
                    T.op("act", lambda e: e.copy(out=dstv, in_=srcv), r=[f"psb{bk}"], w=["olT"])
                else:
                    T.op("dve", lambda e: e.tensor_copy(out=dstv, in_=srcv), r=[f"psb{bk}"], w=["olT"])

        n_it = len(items)
        for i in range(n_it + 4):
            if i < n_it:
                stG(i)
            if 0 <= i - 2 < n_it:
                stA1(i - 2)
            if 0 <= i - 3 < n_it:
                stA2(i - 3)
            if 0 <= i - 4 < n_it:
                stB(i - 4)
        pv = rf.next()
        for h in range(8):
            for cc in range(2):
                mm(psf[pv][:, h * 64:(h + 1) * 64], olT[:, cc * 1024 + h * 128:cc * 1024 + (h + 1) * 128], wuv[:, cc, h * 64:(h + 1) * 64], cc == 0, cc == 1, ["olT", "wuv"], [f"psf{pv}"])
        T.op("act", lambda e: e.copy(out=otok[:, 0, :], in_=psf[pv][:, 0:512]), r=[f"psf{pv}"], w=["otok.0"])

    try:
        chk("setup")
        order = []
        for seq in range(NSEQ):
            order.append(("meta", seq, 0))
            for ti in range(NT):
                order.append(("prompt", seq, ti))
        order.append(("sample", 0, 0))
        for k, (kd_, sq_, ti_) in enumerate(order[:-1]):
            tile(kd_, sq_, ti_, k % 2, order[k + 1], k == 0)
            chk("meta" if kd_ == "meta" else "tile0")
        chk("prompt")
    except _Stop:
        T.finish("pool")
        return nc, T
    T.barrier()
    cd = Carver()
    dec["pgc32"] = [cd.f32(1024) for _ in range(3)]
    dec["pgk32"] = [cd.f32(128) for _ in range(3)]
    dec["pgb"] = [cd.bf(4 * 290).rearrange("p (j c) -> p j c", j=4) for _ in range(3)]
    dec["pgT"] = [cd.bf(1024) for _ in range(2)]
    dec["kpT"] = [cd.bf(512) for _ in range(2)]
    dec["Pt"] = [cd.bf(256) for _ in range(2)]
    dec["qlat"] = cd.bf(2048)
    dec["olT"] = cd.bf(2048)
    dec["oln"] = cd.bf(256)
    dec["Ps"] = cd.bf(64)
    kpTn = cd.bf(128)
    wukT = cd.bf(8 * 256)[0:64, :].rearrange("p (h c) -> p h c", h=8)
    for cc in range(2):
        bk = rb.next()
        for h in range(8):
            T.op("pe", lambda e: e.transpose(out=psb[bk][0:64, h * 128:(h + 1) * 128], in_=wuk[:, cc, h * 64:(h + 1) * 64], identity=idb[:]),
                 r=["wuk", "idb"], w=[f"psb{bk}"])
        T.op("dve", lambda e: e.tensor_copy(out=wukT[:, :, cc * 128:(cc + 1) * 128], in_=psb[bk][0:64, :].rearrange("p (h c) -> p h c", h=8)),
             r=[f"psb{bk}"], w=["wukT"])
    pts = cd.i32(16 * NPG)
    dec["idx"] = cd.i32(16 * NPG)
    for s in range(3):
        T.op("pool", lambda e: e.memset(dec["pgb"][s][:, :, 288:290], 1.0), w=[f"pgb.{s}"])
    for s in range(2):
        T.op("pool", lambda e: e.memset(dec["kpT"][s][0:64, :], 0.0), w=[f"kpT.{s}"])
    T.op("pool", lambda e: e.memset(kpTn[0:64, :], 0.0), w=["kpTn"])
    T.dma("sp", lambda e: e.dma_start(out=pts, in_=ptab.partition_broadcast(128)), "l_pts", w=["pts"])
    NGT = 16 * NPG // 4
    for q in range(4):
        T.op("dve", lambda e: e.tensor_scalar(out=dec["idx"][32 * q:32 * q + 32, 0:NGT], in0=pts.rearrange("p (G q) -> p G q", q=4)[32 * q:32 * q + 32, :, q],
                                              scalar1=32.0, scalar2=iota[32 * q:32 * q + 32, 0:1], op0=ALU.mult, op1=ALU.add), r=["pts", "iota"], w=["idx"])
    tile("sample", 0, 0, (len(order) - 1) % 2, None, False)
    T.finish("pool")
    return nc, T


_CACHE = {}


def _consts(SEQ, NPG):
    inv = (10000.0 ** (-2.0 * np.arange(16, dtype=np.float32) / 32.0)).astype(np.float32)

    def tab(pos):
        ang = pos.astype(np.float32)[:, None] * inv[None, :]
        return np.concatenate([np.cos(ang), np.sin(ang)], axis=1).astype(np.float32)

    NBLK = SEQ // 128
    rp = np.zeros((128, NBLK * 32), np.float32)
    for j in range(NBLK):
        rp[:, j * 32:(j + 1) * 32] = tab(16 + 128 * j + np.arange(128))
    rm = np.zeros((128, 32), np.float32)
    rm[0:16] = tab(np.arange(16))
    rs = tab(NPG * 128 + (np.arange(128) % 8))
    k = np.arange(128)
    tri = (k[:, None] <= k[None, :]).astype(np.float32)
    dm = np.zeros((128, 16, 8, 8), np.float32)
    for b in range(16):
        for s in range(8):
            dm[b * 8:b * 8 + s + 1, b, :, s] = 1.0
    return {"c_ident": np.eye(128, dtype=np.float32), "c_tri": tri, "c_dmask": dm.reshape(128, 1024),
            "c_rope_p": rp, "c_rope_m": rm, "c_rope_s": rs, "c_iota": (np.arange(128) % 32).astype(np.float32).reshape(128, 1)}


def make_in_maps(inputs, n_cores, NSEQ, SEQ, NPG, NPOOL):
    f = lambda a: np.ascontiguousarray(np.asarray(a, dtype=np.float32))
    shared = {
        "meta": f(inputs["meta_tokens"]),
        "cckv": f(inputs["cache_ckv"]).reshape(NPOOL * 128, 256),
        "ckpe": f(inputs["cache_kpe"]).reshape(NPOOL * 128, 32),
        "lneg": f(inputs["ln_emb_g"]).reshape(1, 1024), "lneb": f(inputs["ln_emb_b"]).reshape(1, 1024),
        "ln1g": f(inputs["ln1_g"]).reshape(1, 1024), "ln1b": f(inputs["ln1_b"]).reshape(1, 1024),
        "ln2g": f(inputs["ln2_g"]).reshape(1, 1024), "ln2b": f(inputs["ln2_b"]).reshape(1, 1024),
        "qng": f(inputs["q_norm_g"]).reshape(1, 384), "kvg": f(inputs["kv_norm_g"]).reshape(1, 256),
        "convw": f(inputs["conv_w"]).reshape(3, 512),
        "w_in": f(inputs["w_in"]).reshape(1024, D_IN), "w_uq": f(inputs["w_uq"]).reshape(384, 768),
        "w_uk": f(inputs["w_uk"]).reshape(256, 512), "w_uv": f(inputs["w_uv"]).reshape(256, 512),
        "w_om": f(inputs["w_o_mla"]).reshape(512, 1024), "w_co": f(inputs["w_conv_out"]).reshape(512, 1024),
        "w_o": f(inputs["w_o"]).reshape(1024, 1024), "w_up": f(inputs["w_up"]).reshape(1024, 4096),
        "w_dn": f(inputs["w_down"]).reshape(4096, 1024),
    }
    shared.update(_consts(SEQ, NPG))
    xp = f(inputs["x_prompt"]); xs = f(inputs["x_sample"]); sc = f(inputs["state_conv"])[0]
    pt = np.ascontiguousarray(np.asarray(inputs["page_table"], dtype=np.int32))
    maps = []
    for c in range(n_cores):
        m = dict(shared)
        m["x_p"] = np.ascontiguousarray(xp[c * NSEQ:(c + 1) * NSEQ])
        m["x_s"] = np.ascontiguousarray(xs[c * 16:(c + 1) * 16]).reshape(128, 1024)
        m["sconv"] = np.ascontiguousarray(sc[c * 16:(c + 1) * 16]).reshape(32, 512)
        m["ptab"] = np.ascontiguousarray(pt[c * 16:(c + 1) * 16]).reshape(1, 16 * NPG)
        maps.append(m)
    return maps


def gather_outputs(results, n_cores, NSEQ, SEQ):
    L = SEQ + 16
    cat = lambda k: np.concatenate([np.asarray(r[k], dtype=np.float32) for r in results], axis=0)
    y_p = cat("y_p").reshape(n_cores * NSEQ, SEQ, 1024)
    y_s = cat("y_s").reshape(n_cores * 16, 8, 1024)
    ckv_p = cat("ckv_p").reshape(1, n_cores * NSEQ, L, 256)
    kpe_p = cat("kpe_p").reshape(1, n_cores * NSEQ, L, 32)
    conv_p = cat("conv_p").reshape(1, n_cores * NSEQ, 2, 512)
    ckv_s = cat("ckv_s").reshape(1, n_cores * 16, 8, 256)
    kpe_s = cat("kpe_s").reshape(1, n_cores * 16, 8, 32)
    conv_s = cat("conv_s").reshape(1, n_cores * 16, 2, 512)
    return (y_p, y_s, ckv_p, kpe_p, conv_p, ckv_s, kpe_s, conv_s)


def kernel(**inputs):
    n_cores = 8
    B, SEQ, _ = inputs["x_prompt"].shape
    NSEQ = B // n_cores
    NPG = inputs["page_table"].shape[1]
    NPOOL = inputs["cache_ckv"].shape[1]
    key = (NSEQ, SEQ, NPG, NPOOL)
    if key not in _CACHE:
        _CACHE[key] = build(NSEQ, SEQ, NPG, NPOOL)[0]
    nc = _CACHE[key]
    maps = make_in_maps(inputs, n_cores, NSEQ, SEQ, NPG, NPOOL)
    res = run_bass_kernel_spmd(nc, maps, core_ids=list(range(n_cores)))
    return gather_outputs(res.results, n_cores, NSEQ, SEQ)
```

```python
import numpy as np
import concourse.bass as bass
import concourse.mybir as mybir
from concourse.bass_utils import run_bass_kernel_spmd

F32 = mybir.dt.float32
BF16 = mybir.dt.bfloat16
I32 = mybir.dt.int32
ALU = mybir.AluOpType
AF = mybir.ActivationFunctionType

ALPHA = 2.0 ** 0.25
SM_SCALE = 96.0 ** -0.5
D_IN = 4256


class Trk:
    def __init__(self, nc):
        self.nc = nc
        self.engs = {"pe": nc.tensor, "act": nc.scalar, "dve": nc.vector,
                     "pool": nc.gpsimd, "sp": nc.sync}
        self.sem = {k: nc.alloc_semaphore("s_" + k) for k in self.engs}
        self.cnt = {k: 0 for k in self.engs}
        self.seen = {k: {} for k in self.engs}
        self.lastw = {}
        self.readers = {}
        self.dsem = {}
        self.dcnt = {}
        self.waited = {}
        self.nins = 0

    def _semh(self, sk):
        return self.sem[sk] if sk in self.sem else self.dsem[sk]

    def _deps(self, e, r, w):
        best = {}
        def add(t):
            sk, v = t
            if sk == "pe" and e == "pe":
                return
            if v > best.get(sk, 0):
                best[sk] = v
        for b in r:
            if b in self.lastw:
                add(self.lastw[b])
        for b in w:
            if b in self.lastw:
                add(self.lastw[b])
            for t in self.readers.get(b, ()):
                add(t)
        for sk, v in best.items():
            if sk in self.dsem:
                cand = [x for x in self.waited.setdefault(sk, []) if x >= v]
                v = min(cand) if cand else self.dcnt[sk]
                if v not in self.waited[sk]:
                    self.waited[sk].append(v)
            if self.seen[e].get(sk, 0) < v:
                self.engs[e].wait_ge(self._semh(sk), v)
                self.seen[e][sk] = v

    def _rec(self, tok, r, w):
        for b in r:
            lst = self.readers.setdefault(b, [])
            lst[:] = [t for t in lst if t[0] != tok[0]]
            lst.append(tok)
        for b in w:
            self.lastw[b] = tok
            self.readers[b] = []

    def op(self, e, fn, r=(), w=()):
        ps = [b for b in r if b.startswith("ps")]
        if ps:
            r = [b for b in r if not b.startswith("ps")]
            w = list(w) + ps
        self._deps(e, r, w)
        ins = fn(self.engs[e])
        self.cnt[e] += 1
        ins.then_inc(self.sem[e], 1)
        self._rec((e, self.cnt[e]), r, w)
        self.nins += 1
        return ins

    def dma(self, q, fn, semkey, r=(), w=()):
        if semkey not in self.dsem:
            self.dsem[semkey] = self.nc.alloc_semaphore("d_" + semkey)
            self.dcnt[semkey] = 0
        self._deps(q, r, w)
        ins = fn(self.engs[q])
        self.dcnt[semkey] += 16
        ins.then_inc(self.dsem[semkey], 16)
        self._rec((semkey, self.dcnt[semkey]), r, w)
        self.nins += 1
        return ins

    def _waitall(self, e):
        for sk, v in list(self.cnt.items()) + list(self.dcnt.items()):
            if sk == e or v == 0:
                continue
            if self.seen[e].get(sk, 0) < v:
                self.engs[e].wait_ge(self._semh(sk), v)
                self.seen[e][sk] = v

    def barrier(self):
        for e in self.engs:
            self._waitall(e)

    def finish(self, q):
        self._waitall(q)


class Rot:
    def __init__(self, ids):
        self.ids = list(ids)
        self.i = 0

    def next(self):
        v = self.ids[self.i % len(self.ids)]
        self.i += 1
        return v


class _Stop(Exception):
    pass


def build(NSEQ, SEQ, NPG, NPOOL, debug=(), stop=None):
    TT = 256
    NT = SEQ // TT
    L = SEQ + 16
    NBLK = SEQ // 128
    NCHV = NBLK + 1
    nc = bass.Bass("TRN2", target_bir_lowering=False)
    T = Trk(nc)

    def din(name, shape, dt=F32):
        return nc.dram_tensor(name, list(shape), dt, kind="ExternalInput").ap()

    def dout(name, shape, dt=F32):
        return nc.dram_tensor(name, list(shape), dt, kind="ExternalOutput").ap()

    def sb(name, shape, dt=F32):
        return nc.alloc_sbuf_tensor(name, list(shape), dt)

    x_p = din("x_p", [NSEQ, SEQ, 1024]); x_s = din("x_s", [128, 1024]); meta = din("meta", [16, 1024])
    cckv = din("cckv", [NPOOL * 128, 256]); ckpe = din("ckpe", [NPOOL * 128, 32])
    cckv4 = cckv.rearrange("(n r) c -> n (r c)", r=4); ckpe4 = ckpe.rearrange("(n r) c -> n (r c)", r=4)
    sconv = din("sconv", [32, 512]); ptab = din("ptab", [1, 16 * NPG], I32)
    vec_in = {n: din(n, [1, 1024]) for n in ("lneg", "lneb", "ln1g", "ln1b", "ln2g", "ln2b")}
    qng = din("qng", [1, 384]); kvg = din("kvg", [1, 256]); convw = din("convw", [3, 512])
    w_in = din("w_in", [1024, D_IN]); w_uq = din("w_uq", [384, 768]); w_uk = din("w_uk", [256, 512])
    w_uv = din("w_uv", [256, 512]); w_om = din("w_om", [512, 1024]); w_co = din("w_co", [512, 1024])
    w_o = din("w_o", [1024, 1024]); w_up = din("w_up", [1024, 4096]); w_dn = din("w_dn", [4096, 1024])
    c_ident = din("c_ident", [128, 128]); c_tri = din("c_tri", [128, 128]); c_dmask = din("c_dmask", [128, 1024])
    c_rope_p = din("c_rope_p", [128, NBLK * 32]); c_rope_m = din("c_rope_m", [128, 32]); c_rope_s = din("c_rope_s", [128, 32])
    c_iota = din("c_iota", [128, 1])

    y_p = dout("y_p", [NSEQ, SEQ, 1024]); y_s = dout("y_s", [128, 1024])
    ckv_p = dout("ckv_p", [NSEQ, L, 256]); kpe_p = dout("kpe_p", [NSEQ, L, 32]); conv_p = dout("conv_p", [NSEQ * 2, 512])
    ckv_s = dout("ckv_s", [128, 256]); kpe_s = dout("kpe_s", [128, 32]); conv_s = dout("conv_s", [32, 512])

    NPIECE = 29
    wscr = nc.dram_tensor("wscr", [NPIECE, 128, 4096], BF16, kind="Internal").ap()

    dbg_outs = {}

    def dbg(name, ap, keys, shape):
        if name in debug:
            o = dout("dbg_" + name, shape)
            T.dma("pool", lambda e: e.dma_start(out=o, in_=ap), "dbg", r=keys)

    idf = sb("idf", [128, 128]); idb = sb("idb", [128, 128], BF16); tri = sb("tri", [128, 128], BF16)
    dmask = sb("dmask", [128, 1024], BF16)
    rope_p = sb("rope_p", [128, NBLK * 32]); rope_m = sb("rope_m", [128, 32]); rope_s = sb("rope_s", [128, 32])
    iota = sb("iota", [128, 1]); eps5 = sb("eps5", [128, 1]); eps6 = sb("eps6", [128, 1])
    vecs = {n: sb("v_" + n, [128, 1024]) for n in vec_in}
    gqB = sb("gqB", [128, 384]); gkvB = sb("gkvB", [128, 256]); cw = sb("cw", [128, 16])
    wuq = sb("wuq", [128, 3, 768], BF16); wuk = sb("wuk", [128, 2, 512], BF16); wuv = sb("wuv", [128, 2, 512], BF16)
    NSLOT = 3
    wslot = [sb(f"wslot{i}", [128, 4096], BF16) for i in range(NSLOT)]
    h32s = [sb("h32a", [128, 2, 1024]), sb("h32b", [128, 2, 1024])]; hTs = [sb("hTa", [128, 8, 256], BF16), sb("hTb", [128, 8, 256], BF16)]
    stats = sb("stats", [128, 12]); mv = sb("mv", [128, 2]); sd = sb("sd", [128, 1]); rs = sb("rs", [128, 1])
    ssq = sb("ssq", [128, 1]); junk = sb("junk", [128, 384], BF16)
    cqn = sb("cqn", [128, 1, 384], BF16); cqnT = sb("cqnT", [128, 3, 256], BF16)
    qs = sb("qs", [128, 768]); qr = sb("qr", [128, 1, 768], BF16); qT = sb("qT", [96, 8, 256], BF16)
    rt = sb("rt", [128, 4, 128])
    kvo = sb("kvo", [128, 2, 288]); kvb = sb("kvb", [128, 2, 290], BF16); ckvT = sb("ckvT", [128, 2, 256], BF16)
    kt = sb("kt", [128, 4, 16])
    cvi = sb("cvb", [32, 512]); cvo = cvi
    gb = sb("gb", [128, 4, 256], BF16); gcs = sb("gcs", [128, 4, 256], BF16); upad = sb("upad", [128, 4, 258])
    ct = sb("ct", [128, 256]); cg = sb("cg", [128, 4, 256], BF16)
    sgt = [sb(f"sgt{i}", [128, 2, 256], BF16) for i in range(2)]
    tm = [sb("tm0", [128, 2, 256], BF16)] * 2
    pTt = [sb(f"pTt{i}", [128, 256], BF16) for i in range(3)]
    rec = sb("rec", [128, 2]); otok = sb("otok", [128, 2, 512], BF16); oT = sb("oT", [128, 4, 256], BF16)
    aT = sb("aT", [128, 32, 256], BF16)
    hb = aT[:, 24:32, :].rearrange("p (b c) t -> p b (c t)", b=2)
    yc = aT[:, 0:8, :]; m1 = aT[:, 8:16, :]; sgm = aT[:, 16:24, :]
    YC = ["aT.0", "aT.1"]; M1 = ["aT.2", "aT.3"]; SGM = ["aT.4", "aT.5"]
    rl = [sb(f"rl{i}", [128, 2, 256], BF16) for i in range(2)]
    ulast = sb("ulast", [128, 32])
    ARENA = 12700
    arena = sb("arena", [128, ARENA])

    psf = [nc.alloc_psum_tensor(f"psf{i}", [128, 512], F32) for i in range(6)]
    psb = [nc.alloc_psum_tensor(f"psb{i}", [128, 1024], BF16) for i in range(2)]
    rf = Rot(range(6)); rb = Rot(range(2))

    class Carver:
        def __init__(self):
            self.off = 0

        def f32(self, n):
            a = arena[:, self.off:self.off + n]
            self.off += n
            assert self.off <= ARENA
            return a

        def bf(self, n):
            return self.f32((n + 1) // 2).bitcast(BF16)

        def i32(self, n):
            return self.f32(n).bitcast(I32)

    def ld(out_ap, in_ap, key, q="sp"):
        T.dma(q, lambda e: e.dma_start(out=out_ap, in_=in_ap), "l_" + key, w=[key])

    ld(idf[:], c_ident, "idf"); ld(rope_p[:], c_rope_p, "rope_p"); ld(rope_m[:], c_rope_m, "rope_m")
    ld(rope_s[:], c_rope_s, "rope_s"); ld(iota[:], c_iota, "iota")
    for n in vec_in:
        ld(vecs[n][:], vec_in[n].partition_broadcast(128), "v_" + n)
    ld(gqB[:], qng.partition_broadcast(128), "gqB"); ld(gkvB[:], kvg.partition_broadcast(128), "gkvB")
    T.op("pool", lambda e: e.memset(eps5[:], 1e-5), w=["eps5"])
    T.op("pool", lambda e: e.memset(eps6[:], 1e-6), w=["eps6"])
    T.op("pool", lambda e: e.memset(kvb[:], 1.0), w=["kvb.0", "kvb.1"])
    T.op("act", lambda e: e.copy(out=idb[:], in_=idf[:]), r=["idf"], w=["idb"])

    cs = Carver()
    stg = [cs.f32(4096) for _ in range(2)]
    stb = [cs.bf(4096) for _ in range(2)]
    ld(stg[0][:, 0:128], c_tri, "stg0")
    T.op("act", lambda e: e.copy(out=tri[:], in_=stg[0][:, 0:128]), r=["stg0"], w=["tri"])
    ld(stg[1][:, 0:1024], c_dmask, "stg1")
    T.op("dve", lambda e: e.tensor_copy(out=dmask[:], in_=stg[1][:, 0:1024]), r=["stg1"], w=["dmask"])
    ld(stg[0][:, 0:2304].rearrange("p (k c) -> p k c", k=3), w_uq.rearrange("(k p) c -> p k c", p=128), "stg0")
    T.op("act", lambda e: e.copy(out=wuq[:].rearrange("p k c -> p (k c)"), in_=stg[0][:, 0:2304]), r=["stg0"], w=["wuq"])
    ld(stg[1][:, 0:1024].rearrange("p (k c) -> p k c", k=2), w_uk.rearrange("(k p) c -> p k c", p=128), "stg1")
    T.op("dve", lambda e: e.tensor_copy(out=wuk[:].rearrange("p k c -> p (k c)"), in_=stg[1][:, 0:1024]), r=["stg1"], w=["wuk"])
    ld(stg[0][:, 0:1024].rearrange("p (k c) -> p k c", k=2), w_uv.rearrange("(k p) c -> p k c", p=128), "stg0")
    T.op("act", lambda e: e.copy(out=wuv[:].rearrange("p k c -> p (k c)"), in_=stg[0][:, 0:1024]), r=["stg0"], w=["wuv"])
    T.op("pool", lambda e: e.memset(cvi[:], 0.0), w=["cvb"])
    ld(cvi[0:3, :], convw, "cvb")
    pb = rf.next()
    for cc in range(4):
        T.op("pe", lambda e: e.transpose(out=psf[pb][:, cc * 4:cc * 4 + 4], in_=cvi[0:4, cc * 128:(cc + 1) * 128], identity=idf[0:4, 0:4]),
             r=["cvb", "idf"], w=[f"psf{pb}"])
    T.op("dve", lambda e: e.tensor_copy(out=cw[:], in_=psf[pb][:, 0:16]), r=[f"psf{pb}"], w=["cw"])

    pieces = {}
    for s_ in range(2):
        T.op("pool", lambda e: e.memset(eps5[:], 1e-5), w=["eps5", f"stg{s_}"] + [f"stg{s_}.{k}" for k in range(8)])

    def add_piece(name, src_ap, kc, ncol):
        idx = len(pieces)
        pieces[name] = (idx, kc, ncol)
        s = idx % 2
        n = kc * ncol
        sk = [f"stg{s}.{k}" for k in range(kc)]
        for k in range(kc):
            T.dma("sp", lambda e: e.dma_start(out=stg[s][:, k * ncol:(k + 1) * ncol], in_=src_ap[:, k, :]), f"l_stg{s}", w=[sk[k]])
        eng = "act" if idx % 2 == 0 else "dve"
        if eng == "act":
            T.op("act", lambda e: e.copy(out=stb[s][:, 0:n], in_=stg[s][:, 0:n]), r=sk, w=[f"stb{s}"])
        else:
            T.op("dve", lambda e: e.tensor_copy(out=stb[s][:, 0:n], in_=stg[s][:, 0:n]), r=sk, w=[f"stb{s}"])
        T.dma("pool", lambda e: e.dma_start(out=wscr[idx, :, 0:n], in_=stb[s][:, 0:n]), f"scrst{s}", r=[f"stb{s}"], w=[f"scr.{name}"])

    w_in_v = w_in.rearrange("(k p) c -> p k c", p=128)
    add_piece("A1", w_in_v[:, :, 0:384], 8, 384)
    add_piece("A2", w_in_v[:, :, 384:672], 8, 288)
    for i in range(7):
        add_piece(f"B{i}", w_in_v[:, :, 672 + 512 * i:672 + 512 * (i + 1)], 8, 512)
    w_up_v = w_up.rearrange("(k p) c -> p k c", p=128)
    w_dn_v = w_dn.rearrange("(k p) c -> p k c", p=128)
    for j in range(8):
        add_piece(f"U{j}", w_up_v[:, :, 512 * j:512 * (j + 1)], 8, 512)
        add_piece(f"D{j}", w_dn_v[:, 4 * j:4 * j + 4, :], 4, 1024)
    w_o_v = w_o.rearrange("(k p) c -> p k c", p=128)
    add_piece("O0", w_o_v[:, :, 0:512], 8, 512)
    add_piece("O1", w_o_v[:, :, 512:1024], 8, 512)
    add_piece("CO", w_co.rearrange("(k p) c -> p k c", p=128), 4, 1024)
    add_piece("OM", w_om.rearrange("(k p) c -> p k c", p=128), 4, 1024)
    assert len(pieces) == NPIECE

    wctr = [0]

    def getw(name):
        idx, kc, ncol = pieces[name]
        s = wctr[0] % NSLOT
        wctr[0] += 1
        n = kc * ncol
        T.dma("sp", lambda e: e.dma_start(out=wslot[s][:, 0:n], in_=wscr[idx, :, 0:n]), f"ws{s}", r=[f"scr.{name}"], w=[f"ws{s}"])
        return wslot[s][:, 0:n].rearrange("p (k c) -> p k c", k=kc), f"ws{s}"

    T.barrier()

    T.marks = []

    def chk(stage):
        T.marks.append((stage, dict(T.cnt)))
        if stop == stage:
            raise _Stop()

    cp = Carver()
    Kst = cp.bf(8 * L)[0:96, :].rearrange("p (h l) -> p h l", h=8)
    Vst = cp.bf(NCHV * 520).rearrange("p (c h d) -> p c h d", c=NCHV, h=8)
    T.op("pool", lambda e: e.memset(Vst[:, :, :, 64:65], 1.0), w=["Vst"])

    def mm(out, lhsT, rhs, start, stop, r, w, **kw):
        T.op("pe", lambda e: e.matmul(out, lhsT=lhsT, rhs=rhs, start=start, stop=stop, **kw), r=r, w=w)

    def tr(out, in_, R_in, r, w):
        T.op("pe", lambda e: e.transpose(out=out, in_=in_, identity=idb[0:R_in, 0:R_in]), r=list(r) + ["idb"], w=w)

    def layernorm(par, R, b, g, bt, want_hb=True):
        xb = h32s[par][0:R, b, :]
        k = f"h32.{par}.{b}"
        T.op("dve", lambda e: e.bn_stats(out=stats[0:R, 0:6], in_=xb[:, 0:512]), r=[k], w=["stats.a"])
        T.op("dve", lambda e: e.bn_stats(out=stats[0:R, 6:12], in_=xb[:, 512:1024]), r=[k], w=["stats.b"])
        T.op("dve", lambda e: e.bn_aggr(out=mv[0:R, :], in_=stats[0:R, :]), r=["stats.a", "stats.b"], w=["mv"])
        T.op("act", lambda e: e.activation(out=sd[0:R, :], in_=mv[0:R, 1:2], func=AF.Sqrt, bias=eps5[0:R, :], scale=1.0), r=["mv", "eps5"], w=["sd"])
        T.op("dve", lambda e: e.reciprocal(out=rs[0:R, :], in_=sd[0:R, :]), r=["sd"], w=["rs"])
        T.op("dve", lambda e: e.scalar_tensor_tensor(out=xb, in0=xb, scalar=mv[0:R, 0:1], in1=vecs[g][0:R, :], op0=ALU.subtract, op1=ALU.mult),
             r=[k, "mv", "v_" + g], w=[k])
        T.op("dve", lambda e: e.scalar_tensor_tensor(out=xb, in0=xb, scalar=rs[0:R, :], in1=vecs[bt][0:R, :], op0=ALU.mult, op1=ALU.add),
             r=[k, "rs", "v_" + bt], w=[k])
        if want_hb:
            T.op("act", lambda e: e.copy(out=hb[0:R, b, :], in_=xb), r=[k], w=[f"aT.{6 + b}"])

    def to_hT(par, R, b):
        bk = rb.next()
        for c in range(8):
            tr(psb[bk][:, c * 128:c * 128 + R], hb[0:R, b, c * 128:(c + 1) * 128], R, [f"aT.{6 + b}"], [f"psb{bk}"])
        T.op("dve", lambda e: e.tensor_copy(out=hTs[par][:, :, b * R:(b + 1) * R],
                                            in_=psb[bk][:, :].rearrange("p (c t) -> p c t", c=8)[:, :, 0:R]),
             r=[f"psb{bk}"], w=[f"hT.{par}.{b}"])

    def prologue(kind, seq, ti, par):
        R = 16 if kind == "meta" else 128
        nb = 2 if kind == "prompt" else 1
        H = h32s[par]
        hkeys = [f"h32.{par}.{b}" for b in range(nb)]
        if kind == "prompt":
            src = x_p[seq, ti * TT:(ti + 1) * TT, :].rearrange("(n p) d -> p n d", p=128)
            T.dma("sp", lambda e: e.dma_start(out=H[:, :, :], in_=src), f"x{par}", w=hkeys)
        elif kind == "meta":
            T.dma("sp", lambda e: e.dma_start(out=H[0:16, 0, :], in_=meta), f"x{par}", w=hkeys)
        else:
            T.dma("sp", lambda e: e.dma_start(out=H[:, 0, :], in_=x_s), f"x{par}", w=hkeys)
        for b in range(nb):
            layernorm(par, R, b, "lneg", "lneb")
            to_hT(par, R, b)

    def rms_scale(R, ps_ap, n, pkey):
        T.op("act", lambda e: e.activation(out=junk[0:R, 0:n], in_=ps_ap, func=AF.Square, accum_out=ssq[0:R, :]), r=[pkey], w=["junk", "ssq"])
        T.op("act", lambda e: e.activation(out=sd[0:R, :], in_=ssq[0:R, :], func=AF.Sqrt, bias=eps6[0:R, :], scale=1.0 / n), r=["ssq", "eps6"], w=["sd"])
        T.op("dve", lambda e: e.reciprocal(out=rs[0:R, :], in_=sd[0:R, :]), r=["sd"], w=["rs"])

    def rope_ops(R, x1, x2, o1, o2, cosb, sinb, t, rkeys, wkeys, tkey):
        tk = [f"{tkey}{i}" for i in range(4)]
        T.op("dve", lambda e: e.tensor_tensor(out=t[0], in0=x1, in1=cosb, op=ALU.mult), r=rkeys, w=[tk[0]])
        T.op("dve", lambda e: e.tensor_tensor(out=t[1], in0=x2, in1=sinb, op=ALU.mult), r=rkeys, w=[tk[1]])
        T.op("dve", lambda e: e.tensor_tensor(out=t[2], in0=x2, in1=cosb, op=ALU.mult), r=rkeys, w=[tk[2]])
        T.op("dve", lambda e: e.tensor_tensor(out=t[3], in0=x1, in1=sinb, op=ALU.mult), r=rkeys, w=[tk[3]])
        T.op("dve", lambda e: e.tensor_tensor(out=o1, in0=t[0], in1=t[1], op=ALU.subtract), r=[tk[0], tk[1]], w=wkeys)
        T.op("dve", lambda e: e.tensor_tensor(out=o2, in0=t[2], in1=t[3], op=ALU.add), r=[tk[2], tk[3]], w=wkeys)

    yst_ctr = [0]

    def tile(kind, seq, ti, par, nxt, first):
        is_meta = kind == "meta"
        is_samp = kind == "sample"
        R = 16 if is_meta else 128
        nb = 2 if kind == "prompt" else 1
        Tt = nb * R
        hkeys = [f"h32.{par}.{b}" for b in range(nb)]
        hTk = [f"hT.{par}.{b}" for b in range(nb)]
        H = h32s[par]; HT = hTs[par]
        if first:
            prologue(kind, seq, ti, par)
        if kind == "prompt" and seq == 0 and ti == 0:
            dbg("h0", H[:, 0, :], [f"h32.{par}.0"], [128, 1024])

        def ropetab(b):
            if is_meta:
                return rope_m[0:R, :]
            if is_samp:
                return rope_s[0:R, :]
            j = ti * 2 + b
            return rope_p[0:R, j * 32:(j + 1) * 32]

        def chain():
            if not is_meta:
                for b in range(nb):
                    A1, k1 = getw("A1")
                    p1 = rC.next()
                    for kc in range(8):
                        mm(psf[p1][0:R, 0:384], HT[:, kc, b * R:(b + 1) * R], A1[:, kc, :], kc == 0, kc == 7, [hTk[b], k1], [f"psf{p1}"])
                    yield
                    rms_scale(R, psf[p1][0:R, 0:384], 384, f"psf{p1}")
                    T.op("dve", lambda e: e.scalar_tensor_tensor(out=cqn[0:R, 0, :], in0=psf[p1][0:R, 0:384], scalar=rs[0:R, :], in1=gqB[0:R, :],
                                                                 op0=ALU.mult, op1=ALU.mult), r=[f"psf{p1}", "rs", "gqB"], w=["cqn"])
                    yield
                    bk = rb.next()
                    for c in range(3):
                        tr(psb[bk][:, c * 128:c * 128 + R], cqn[0:R, 0, c * 128:(c + 1) * 128], R, ["cqn"], [f"psb{bk}"])
                    T.op("act", lambda e: e.copy(out=cqnT[:, :, b * R:(b + 1) * R], in_=psb[bk][:, 0:384].rearrange("p (c t) -> p c t", c=3)[:, :, 0:R]),
                         r=[f"psb{bk}"], w=[f"cqnT.{b}"])
                    yield
                    p2 = rC.next(); p3 = rC.next()
                    for kc in range(3):
                        mm(psf[p2][0:R, 0:512], cqnT[:, kc, b * R:(b + 1) * R], wuq[:, kc, 0:512], kc == 0, kc == 2, [f"cqnT.{b}", "wuq"], [f"psf{p2}"])
                    for kc in range(3):
                        mm(psf[p3][0:R, 0:256], cqnT[:, kc, b * R:(b + 1) * R], wuq[:, kc, 512:768], kc == 0, kc == 2, [f"cqnT.{b}", "wuq"], [f"psf{p3}"])
                    yield
                    T.op("act", lambda e: e.copy(out=qs[0:R, 0:512], in_=psf[p2][0:R, 0:512]), r=[f"psf{p2}"], w=["qs.a"])
                    T.op("dve", lambda e: e.tensor_copy(out=qs[0:R, 512:768], in_=psf[p3][0:R, 0:256]), r=[f"psf{p3}"], w=["qs.b"])
                    q3 = qs[0:R, :].rearrange("p (h c) -> p h c", h=8)
                    qr3 = qr[0:R, 0, :].rearrange("p (h c) -> p h c", h=8)
                    tab = ropetab(b)
                    cosb = tab[:, 0:16].unsqueeze(1).to_broadcast([R, 8, 16])
                    sinb = tab[:, 16:32].unsqueeze(1).to_broadcast([R, 8, 16])
                    T.op("act", lambda e: e.copy(out=qr3[:, :, 0:64], in_=q3[:, :, 0:64]), r=["qs.a", "qs.b"], w=["qr"])
                    tt = [rt[0:R, i, :].rearrange("p (h c) -> p h c", h=8) for i in range(4)]
                    rope_ops(R, q3[:, :, 64:80], q3[:, :, 80:96], qr3[:, :, 64:80], qr3[:, :, 80:96], cosb, sinb, tt, ["qs.a", "qs.b", "rope_p", "rope_s"], ["qr"], "rt")
                    yield
                    bk = rb.next()
                    for h in range(8):
                        tr(psb[bk][0:96, h * 128:h * 128 + R], qr[0:R, 0, h * 96:(h + 1) * 96], R, ["qr"], [f"psb{bk}"])
                    T.op("act", lambda e: e.copy(out=qT[:, :, b * R:(b + 1) * R], in_=psb[bk][0:96, :].rearrange("p (h t) -> p h t", h=8)[:, :, 0:R]),
                         r=[f"psb{bk}"], w=[f"qT.{b}"])

            for b in range(nb):
                A2, k2 = getw("A2")
                pa = rC.next()
                for kc in range(8):
                    mm(psf[pa][0:R, 0:288], HT[:, kc, b * R:(b + 1) * R], A2[:, kc, :], kc == 0, kc == 7, [hTk[b], k2], [f"psf{pa}"])
                yield
                rms_scale(R, psf[pa][0:R, 0:256], 256, f"psf{pa}")
                T.op("dve", lambda e: e.scalar_tensor_tensor(out=kvo[0:R, b, 0:256], in0=psf[pa][0:R, 0:256], scalar=rs[0:R, :], in1=gkvB[0:R, :],
                                                             op0=ALU.mult, op1=ALU.mult), r=[f"psf{pa}", "rs", "gkvB"], w=[f"kvo.{b}"])
                tab = ropetab(b)
                tt = [kt[0:R, i, :] for i in range(4)]
                rope_ops(R, psf[pa][0:R, 256:272], psf[pa][0:R, 272:288], kvo[0:R, b, 256:272], kvo[0:R, b, 272:288],
                         tab[:, 0:16], tab[:, 16:32], tt, [f"psf{pa}", "rope_p", "rope_m", "rope_s"], [f"kvo.{b}"], "kt")
                if is_meta:
                    o1, o2 = ckv_p[seq, 0:16, :], kpe_p[seq, 0:16, :]
                elif is_samp:
                    o1, o2 = ckv_s, kpe_s
                else:
                    r0 = 16 + ti * TT + b * 128
                    o1, o2 = ckv_p[seq, r0:r0 + 128, :], kpe_p[seq, r0:r0 + 128, :]
                yield
                T.dma("pool", lambda e: e.dma_start(out=o1, in_=kvo[0:R, b, 0:256]), f"kvst{b}", r=[f"kvo.{b}"])
                T.dma("pool", lambda e: e.dma_start(out=o2, in_=kvo[0:R, b, 256:288]), f"kvst{b}", r=[f"kvo.{b}"])
                T.op("act", lambda e: e.copy(out=kvb[0:R, b, 0:288], in_=kvo[0:R, b, :]), r=[f"kvo.{b}"], w=[f"kvb.{b}"])
                yield
                bk = rb.next()
                for c in range(2):
                    tr(psb[bk][:, c * 128:c * 128 + R], kvb[0:R, b, c * 128:(c + 1) * 128], R, [f"kvb.{b}"], [f"psb{bk}"])
                tr(psb[bk][0:96, 256:256 + R], kvb[0:R, b, 192:288], R, [f"kvb.{b}"], [f"psb{bk}"])
                T.op("dve", lambda e: e.tensor_copy(out=ckvT[:, :, b * R:(b + 1) * R], in_=psb[bk][:, 0:256].rearrange("p (c t) -> p c t", c=2)[:, :, 0:R]),
                     r=[f"psb{bk}"], w=[f"ckvT.{b}"])
                if is_samp:
                    T.op("act", lambda e: e.copy(out=kpTn[64:96, 0:128], in_=psb[bk][64:96, 256:384]), r=[f"psb{bk}"], w=["kpTn"])
                else:
                    kc0 = 0 if is_meta else 16 + ti * TT + b * 128
                    T.op("act", lambda e: e.copy(out=Kst[64:96, :, kc0:kc0 + R], in_=psb[bk][64:96, 256:256 + R].unsqueeze(1).to_broadcast([32, 8, R])),
                         r=[f"psb{bk}"], w=[f"K.{0 if is_meta else 1 + ti}"])
            if not is_samp:
                kkey = f"K.{0 if is_meta else 1 + ti}"
                kc0 = 0 if is_meta else 16 + ti * TT
                ckk = [f"ckvT.{b}" for b in range(nb)]
                for h in range(8):
                    yield
                    pk = rC.next()
                    for c in range(2):
                        mm(psf[pk][0:64, 0:Tt], wuk[:, c, h * 64:(h + 1) * 64], ckvT[:, c, 0:Tt], c == 0, c == 1, ckk + ["wuk"], [f"psf{pk}"])
                    if h % 2 == 0:
                        T.op("act", lambda e: e.copy(out=Kst[0:64, h, kc0:kc0 + Tt], in_=psf[pk][0:64, 0:Tt]), r=[f"psf{pk}"], w=[kkey])
                    else:
                        T.op("dve", lambda e: e.tensor_copy(out=Kst[0:64, h, kc0:kc0 + Tt], in_=psf[pk][0:64, 0:Tt]), r=[f"psf{pk}"], w=[kkey])
                for b in range(nb):
                    yield
                    pv = rC.next()
                    vch = 0 if is_meta else 1 + ti * 2 + b
                    for c in range(2):
                        mm(psf[pv][0:R, 0:512], ckvT[:, c, b * R:(b + 1) * R], wuv[:, c, :], c == 0, c == 1, [f"ckvT.{b}", "wuv"], [f"psf{pv}"])
                    T.op("dve", lambda e: e.tensor_copy(out=Vst[0:R, vch, :, 0:64], in_=psf[pv][0:R, 0:512].rearrange("p (h d) -> p h d", h=8)),
                         r=[f"psf{pv}"], w=["Vst"])

            yield

        def bulk():
            def bpair(Bp, kB, j0):
                pbk = rBk.next()
                for j2 in range(2):
                    j = j0 + j2
                    for kc in range(8):
                        mm(psf[pbk][:, j2 * 256:j2 * 256 + Tt], Bp[:, kc, j * 128:(j + 1) * 128], HT[:, kc, 0:Tt], kc == 0, kc == 7, hTk + [kB], [f"psf{pbk}"])
                return pbk, psf[pbk][:, :].rearrange("p (a t) -> p a t", a=2)[:, :, 0:Tt]

            if not is_meta:
                yield
                Bp, kB = getw("B0")
                for j0 in (0, 2):
                    yield
                    pbk, pv2 = bpair(Bp, kB, j0)
                    T.op("act", lambda e: e.copy(out=gb[:, j0:j0 + 2, 0:Tt], in_=pv2), r=[f"psf{pbk}"], w=["gb"])
            yield
            Bp, kB = getw("B1")
            for j0 in (0, 2):
                yield
                pbk, pv2 = bpair(Bp, kB, j0)
                T.op("act", lambda e: e.copy(out=gcs[:, j0:j0 + 2, 0:Tt], in_=pv2), r=[f"psf{pbk}"], w=["gcs"])
            yield
            Bp, kB = getw("B2")
            if is_samp:
                T.dma("sp", lambda e: e.dma_start(out=cvi[:, :], in_=sconv), "l_cvi", w=["cvb"])
                pc = rBk.next()
                for cc in range(4):
                    T.op("pe", lambda e: e.transpose(out=psf[pc][:, cc * 32:(cc + 1) * 32], in_=cvi[0:32, cc * 128:(cc + 1) * 128], identity=idf[0:32, 0:32]),
                         r=["cvb", "idf"], w=[f"psf{pc}"])
                for cc in range(4):
                    uv = upad[:, cc, 0:160].rearrange("p (b s) -> p b s", s=10)
                    T.op("dve", lambda e: e.tensor_copy(out=uv[:, :, 0:2], in_=psf[pc][:, cc * 32:(cc + 1) * 32].rearrange("p (b s) -> p b s", s=2)),
                         r=[f"psf{pc}"], w=[f"upad.{cc}"])
            for j0 in (0, 2):
                yield
                pbk, pv2 = bpair(Bp, kB, j0)
                for j2 in range(2):
                    cc = j0 + j2
                    uk = f"upad.{cc}"
                    if is_samp:
                        uv = upad[:, cc, 0:160].rearrange("p (b s) -> p b s", s=10)
                        T.op("dve", lambda e: e.tensor_tensor(out=uv[:, :, 2:10], in0=psf[pbk][:, j2 * 256:j2 * 256 + 128].rearrange("p (b s) -> p b s", s=8),
                                                              in1=gcs[:, cc, 0:128].rearrange("p (b s) -> p b s", s=8), op=ALU.mult),
                             r=[f"psf{pbk}", "gcs"], w=[uk])
                        U = [uv[:, :, k:k + 8] for k in range(3)]
                        ctv = ct[:, 0:128].rearrange("p (b s) -> p b s", s=8)
                        cgv = cg[:, cc, 0:128].rearrange("p (b s) -> p b s", s=8)
                        gbv = gb[:, cc, 0:128].rearrange("p (b s) -> p b s", s=8)
                    else:
                        T.op("dve", lambda e: e.tensor_tensor(out=upad[:, cc, 2:2 + Tt], in0=psf[pbk][:, j2 * 256:j2 * 256 + Tt], in1=gcs[:, cc, 0:Tt], op=ALU.mult),
                             r=[f"psf{pbk}", "gcs"], w=[uk])
                        U = [upad[:, cc, k:k + Tt] for k in range(3)]
                        ctv = ct[:, 0:Tt]; cgv = cg[:, cc, 0:Tt]; gbv = gb[:, cc, 0:Tt]
                    if not is_meta:
                        T.op("dve", lambda e: e.tensor_scalar(out=ctv, in0=U[0], scalar1=cw[:, cc * 4:cc * 4 + 1], scalar2=None, op0=ALU.mult), r=[uk, "cw"], w=["ct"])
                        T.op("dve", lambda e: e.scalar_tensor_tensor(out=ctv, in0=U[1], scalar=cw[:, cc * 4 + 1:cc * 4 + 2], in1=ctv, op0=ALU.mult, op1=ALU.add),
                             r=[uk, "cw", "ct"], w=["ct"])
                        T.op("dve", lambda e: e.scalar_tensor_tensor(out=ctv, in0=U[2], scalar=cw[:, cc * 4 + 2:cc * 4 + 3], in1=ctv, op0=ALU.mult, op1=ALU.add),
                             r=[uk, "cw", "ct"], w=["ct"])
                        T.op("pool", lambda e: e.tensor_tensor(out=cgv, in0=ctv, in1=gbv, op=ALU.mult), r=["ct", "gb"], w=["cg"])
                    if is_samp:
                        T.op("dve", lambda e: e.tensor_copy(out=ulast[:, :].rearrange("p (b s) -> p b s", s=2), in_=uv[:, :, 8:10]), r=[uk], w=["ulast"])
                        pcc = rBk.next()
                        T.op("pe", lambda e: e.transpose(out=psf[pcc][0:32, 0:128], in_=ulast[:, :], identity=idf[:, :]), r=["ulast", "idf"], w=[f"psf{pcc}"])
                        T.op("act", lambda e: e.copy(out=cvo[0:32, cc * 128:(cc + 1) * 128], in_=psf[pcc][0:32, 0:128]), r=[f"psf{pcc}"], w=["cvb"])
                    else:
                        T.op("dve", lambda e: e.tensor_copy(out=upad[:, cc, 0:2], in_=upad[:, cc, Tt:Tt + 2]), r=[uk], w=[uk])
                        if kind == "prompt" and ti == NT - 1:
                            pcc = rBk.next()
                            T.op("pe", lambda e: e.transpose(out=psf[pcc][0:2, 0:128], in_=upad[:, cc, 0:2], identity=idf[:, :]), r=[uk, "idf"], w=[f"psf{pcc}"])
                            T.op("act", lambda e: e.copy(out=cvo[0:2, cc * 128:(cc + 1) * 128], in_=psf[pcc][0:2, 0:128]), r=[f"psf{pcc}"], w=["cvb"])
            if is_samp:
                T.dma("pool", lambda e: e.dma_start(out=conv_s, in_=cvo[0:32, :]), "cvst", r=["cvb"])
            elif kind == "prompt" and ti == NT - 1:
                T.dma("pool", lambda e: e.dma_start(out=conv_p[2 * seq:2 * seq + 2, :], in_=cvo[0:2, :]), "cvst", r=["cvb"])
            if is_meta:
                return

            yield
            CO, kco = getw("CO")
            for oc in (0, 2, 4, 6):
                yield
                pbk = rBk.next()
                for j2 in range(2):
                    for kc in range(4):
                        mm(psf[pbk][:, j2 * 256:j2 * 256 + Tt], CO[:, kc, (oc + j2) * 128:(oc + j2 + 1) * 128], cg[:, kc, 0:Tt], kc == 0, kc == 3, ["cg", kco], [f"psf{pbk}"])
                pv2 = psf[pbk][:, :].rearrange("p (a t) -> p a t", a=2)[:, :, 0:Tt]
                T.op("dve", lambda e: e.tensor_copy(out=yc[:, oc:oc + 2, 0:Tt], in_=pv2), r=[f"psf{pbk}"], w=YC)
            si = 0
            for pi in (3, 4):
                yield
                Bp, kB = getw(f"B{pi}")
                for j0 in (0, 2):
                    oc = (pi - 3) * 4 + j0
                    yield
                    pbk, pv2 = bpair(Bp, kB, j0)
                    sg = sgt[si % 2]; sk = f"sgt{si % 2}"; si += 1
                    T.op("act", lambda e: e.activation(out=sg[:, :, 0:Tt], in_=pv2, func=AF.Sigmoid), r=[f"psf{pbk}"], w=[sk])
                    T.op("pool", lambda e: e.tensor_tensor(out=m1[:, oc:oc + 2, 0:Tt], in0=sg[:, :, 0:Tt], in1=yc[:, oc:oc + 2, 0:Tt], op=ALU.mult), r=[sk] + YC, w=M1)
            for pi in (5, 6):
                yield
                Bp, kB = getw(f"B{pi}")
                for j0 in (0, 2):
                    oc = (pi - 5) * 4 + j0
                    yield
                    pbk, pv2 = bpair(Bp, kB, j0)
                    T.op("act", lambda e: e.activation(out=sgm[:, oc:oc + 2, 0:Tt], in_=pv2, func=AF.Sigmoid), r=[f"psf{pbk}"], w=SGM)

            yield

        rC = Rot([0, 1, 2]); rBk = Rot([3, 4, 5])
        gc, gb_ = chain(), bulk()
        alive = [True, True]
        while alive[0] or alive[1]:
            if alive[1]:
                try:
                    next(gb_)
                except StopIteration:
                    alive[1] = False
            if alive[0]:
                try:
                    next(gc)
                except StopIteration:
                    alive[0] = False
        if is_meta:
            if nxt is not None:
                prologue(nxt[0], nxt[1], nxt[2], 1 - par)
            return

        chk("front")
        if kind == "prompt":
            rS = Rot([0, 1, 2, 3])
            qk = ["qT.0", "qT.1"]
            LA = 2
            items = []
            for h in range(8):
                chunks = [(0, 16, 0, None, "K.0")]
                for c in range(2 * ti):
                    chunks.append((16 + 128 * c, 128, 1 + c, None, f"K.{1 + c // 2}"))
                for jj in range(2):
                    c = 2 * ti + jj
                    chunks.append((16 + 128 * c, 128, 1 + c, jj, f"K.{1 + ti}"))
                for ci, ch in enumerate(chunks):
                    items.append((h, ci == 0, ci == len(chunks) - 1) + ch)
            st = {}

            def emitS(i):
                h, isfirst, islast, k0, nk, vch, jj, kkey = items[i]
                q0 = 128 * jj if jj is not None else 0
                sbk = rS.next()
                mm(psf[sbk][0:nk, q0:Tt], Kst[:, h, k0:k0 + nk], qT[:, h, q0:Tt], True, True, [kkey] + qk, [f"psf{sbk}"])
                pt_ = pTt[i % 3]; pk_ = f"pTt{i % 3}"
                T.op("act", lambda e: e.activation(out=pt_[0:nk, q0:Tt], in_=psf[sbk][0:nk, q0:Tt], func=AF.Exp, scale=SM_SCALE), r=[f"psf{sbk}"], w=[pk_])
                if jj is not None:
                    T.op("pool", lambda e: e.tensor_tensor(out=pt_[:, q0:q0 + 128], in0=pt_[:, q0:q0 + 128], in1=tri[:, :], op=ALU.mult), r=[pk_, "tri"], w=[pk_])

            def emitPV(i):
                h, isfirst, islast, k0, nk, vch, jj, kkey = items[i]
                ob = 4 + h % 2
                Ov = psf[ob][:, 0:130].rearrange("p (a d) -> p a d", a=2)
                pt_ = pTt[i % 3]; pk_ = f"pTt{i % 3}"
                first = isfirst
                for qb in range(jj if jj is not None else 0, 2):
                    last = (jj == qb)
                    mm(Ov[:, qb, :], pt_[0:nk, qb * 128:(qb + 1) * 128], Vst[0:nk, vch, h, :], first, last, [pk_, "Vst"], [f"psf{ob}"], skip_group_check=True)
                    first = False
                if islast:
                    T.op("dve", lambda e: e.reciprocal(out=rec[:, :], in_=Ov[:, :, 64]), r=[f"psf{ob}"], w=["rec"])
                    for qb in range(2):
                        T.op("dve", lambda e: e.tensor_scalar(out=otok[:, qb, h * 64:(h + 1) * 64], in0=Ov[:, qb, 0:64], scalar1=rec[:, qb:qb + 1], scalar2=None, op0=ALU.mult),
                             r=[f"psf{ob}", "rec"], w=[f"otok.{qb}"])

            n_it = len(items)
            for i in range(n_it + LA):
                if i < n_it:
                    emitS(i)
                if i - LA >= 0:
                    emitPV(i - LA)
        else:
            decode_attention()

        chk("attn")
        for b in range(nb):
            bk = rb.next()
            for c in range(4):
                tr(psb[bk][:, c * 128:c * 128 + R], otok[0:R, b, c * 128:(c + 1) * 128], R, [f"otok.{b}"], [f"psb{bk}"])
            T.op("dve", lambda e: e.tensor_copy(out=oT[:, :, b * R:(b + 1) * R], in_=psb[bk][:, 0:512].rearrange("p (c t) -> p c t", c=4)[:, :, 0:R]),
                 r=[f"psb{bk}"], w=[f"oT.{b}"])
        oTk = [f"oT.{b}" for b in range(nb)]
        OM, kom = getw("OM")
        ti_ = 0
        for oc in (0, 2, 4, 6):
            pbk = rf.next()
            for j2 in range(2):
                for kc in range(4):
                    mm(psf[pbk][:, j2 * 256:j2 * 256 + Tt], OM[:, kc, (oc + j2) * 128:(oc + j2 + 1) * 128], oT[:, kc, 0:Tt], kc == 0, kc == 3, oTk + [kom], [f"psf{pbk}"])
            pv2 = psf[pbk][:, :].rearrange("p (a t) -> p a t", a=2)[:, :, 0:Tt]
            tmv = tm[ti_ % 2]; tk = "tm0"; ti_ += 1
            T.op("dve", lambda e: e.tensor_tensor(out=tmv[:, :, 0:Tt], in0=pv2, in1=sgm[:, oc:oc + 2, 0:Tt], op=ALU.mult), r=[f"psf{pbk}"] + SGM, w=[tk])
            T.op("dve", lambda e: e.tensor_tensor(out=m1[:, oc:oc + 2, 0:Tt], in0=tmv[:, :, 0:Tt], in1=m1[:, oc:oc + 2, 0:Tt], op=ALU.add), r=[tk] + M1, w=M1)
        Ow = [getw("O0"), getw("O1")]
        for b in range(nb):
            for half in range(2):
                Oh, koh = Ow[half]
                pm = rf.next()
                for kc in range(8):
                    mm(psf[pm][0:R, 0:512], m1[:, kc, b * R:(b + 1) * R], Oh[:, kc, :], kc == 0, kc == 7, M1 + [koh], [f"psf{pm}"])
                hv = H[0:R, b, half * 512:(half + 1) * 512]
                T.op("dve", lambda e: e.scalar_tensor_tensor(out=hv, in0=hv, scalar=ALPHA, in1=psf[pm][0:R, 0:512], op0=ALU.mult, op1=ALU.add),
                     r=[f"h32.{par}.{b}", f"psf{pm}"], w=[f"h32.{par}.{b}"])
            layernorm(par, R, b, "ln1g", "ln1b")
            to_hT(par, R, b)
        chk("post")
        rU = Rot([4, 5])
        ri = [0]

        def up(j):
            U, ku = getw(f"U{j}")
            for p in range(2):
                pu = rU.next()
                for j2 in range(2):
                    c = 2 * p + j2
                    for kc in range(8):
                        mm(psf[pu][:, j2 * 256:j2 * 256 + Tt], U[:, kc, c * 128:(c + 1) * 128], HT[:, kc, 0:Tt], kc == 0, kc == 7, hTk + [ku], [f"psf{pu}"])
                pv2 = psf[pu][:, :].rearrange("p (a t) -> p a t", a=2)[:, :, 0:Tt]
                rv = rl[ri[0] % 2]; rk = f"rl{ri[0] % 2}"; ri[0] += 1
                T.op("act", lambda e: e.activation(out=rv[:, :, 0:Tt], in_=pv2, func=AF.Relu), r=[f"psf{pu}"], w=[rk])
                fc0 = 4 * j + 2 * p
                T.op("pool", lambda e: e.tensor_tensor(out=aT[:, fc0:fc0 + 2, 0:Tt], in0=rv[:, :, 0:Tt], in1=rv[:, :, 0:Tt], op=ALU.mult), r=[rk], w=[f"aT.{j}"])

        def down(j):
            D, kd = getw(f"D{j}")
            for f in range(4):
                fc = 4 * j + f
                for b in range(nb):
                    for half in range(2):
                        pa = b * 2 + half
                        mm(psf[pa][0:R, 0:512], aT[:, fc, b * R:(b + 1) * R], D[:, f, half * 512:(half + 1) * 512], fc == 0, fc == 31, [f"aT.{j}", kd], [f"psf{pa}"])

        up(0)
        for j in range(8):
            if j + 1 < 8:
                up(j + 1)
            down(j)
            if j == 1 and nxt is not None:
                prologue(nxt[0], nxt[1], nxt[2], 1 - par)
        for b in range(nb):
            for half in range(2):
                pa = b * 2 + half
                hv = H[0:R, b, half * 512:(half + 1) * 512]
                T.op("dve", lambda e: e.scalar_tensor_tensor(out=hv, in0=hv, scalar=ALPHA, in1=psf[pa][0:R, 0:512], op0=ALU.mult, op1=ALU.add),
                     r=[f"h32.{par}.{b}", f"psf{pa}"], w=[f"h32.{par}.{b}"])
            layernorm(par, R, b, "ln2g", "ln2b", want_hb=False)
            if is_samp:
                dst = y_s
            else:
                r0 = ti * TT + b * 128
                dst = y_p[seq, r0:r0 + 128, :]
            T.dma("pool", lambda e: e.dma_start(out=dst, in_=H[0:R, b, :]), f"yst{b}", r=[f"h32.{par}.{b}"])

    dec = {}

    def decode_attention():
        pgc32, pgk32, pgb, pgT, kpT, Pt = dec["pgc32"], dec["pgk32"], dec["pgb"], dec["pgT"], dec["kpT"], dec["Pt"]
        qlat, olT, idx, oln, Ps = dec["qlat"], dec["olT"], dec["idx"], dec["oln"], dec["Ps"]
        for h in range(8):
            for cc in range(2):
                pq = rf.next()
                mm(psf[pq][:, 0:128], wukT[0:64, h, cc * 128:(cc + 1) * 128], qT[0:64, h, 0:128], True, True, ["wukT", "qT.0"], [f"psf{pq}"])
                dstv = qlat[:, cc * 1024:(cc + 1) * 1024].rearrange("p (b x) -> p b x", b=16)[:, :, h * 8:(h + 1) * 8]
                srcv = psf[pq][:, 0:128].rearrange("p (b s) -> p b s", s=8)
                if (h + cc) % 2 == 0:
                    T.op("act", lambda e: e.copy(out=dstv, in_=srcv), r=[f"psf{pq}"], w=["qlat"])
                else:
                    T.op("dve", lambda e: e.tensor_copy(out=dstv, in_=srcv), r=[f"psf{pq}"], w=["qlat"])
        rS = Rot([0, 1, 2, 3])
        NG = NPG // 4
        items = []
        for b in range(16):
            for g in range(NG):
                items.append(("grp", b, g))
            items.append(("self", b, 0))
        slot_of = {}
        sl = 0
        slot3 = {}
        for i, it in enumerate(items):
            if it[0] == "grp":
                slot_of[i] = sl % 2
                slot3[i] = sl % 3
                sl += 1
        sbank = {}

        def qviews(b):
            ql = [qlat[:, cc * 1024 + b * 64:cc * 1024 + (b + 1) * 64] for cc in range(2)]
            qpe = qT[0:96, :, b * 8:(b + 1) * 8]
            return ql, qpe

        def stG(i):
            kind_, b, g = items[i]
            if kind_ != "grp":
                return
            s = slot3[i]
            col = b * NG + g
            T.dma("pool", lambda e: e.indirect_dma_start(out=pgc32[s][:, :], out_offset=None, in_=cckv4,
                                                          in_offset=bass.IndirectOffsetOnAxis(ap=idx[:, col:col + 1], axis=0)),
                  f"pg{s}a", r=["idx"], w=[f"pg32.{s}.a"])
            T.dma("pool", lambda e: e.indirect_dma_start(out=pgk32[s][:, :], out_offset=None, in_=ckpe4,
                                                          in_offset=bass.IndirectOffsetOnAxis(ap=idx[:, col:col + 1], axis=0)),
                  f"pg{s}b", r=["idx"], w=[f"pg32.{s}.b"])

        def stA1(i):
            kind_, b, g = items[i]
            if kind_ != "grp":
                return
            s3 = slot3[i]
            T.op("dve", lambda e: e.tensor_copy(out=pgb[s3][:, :, 0:256], in_=pgc32[s3][:, :].rearrange("p (r c) -> p r c", r=4)), r=[f"pg32.{s3}.a"], w=[f"pgb.{s3}"])
            T.op("act", lambda e: e.copy(out=pgb[s3][:, :, 256:288], in_=pgk32[s3][:, :].rearrange("p (r c) -> p r c", r=4)), r=[f"pg32.{s3}.b"], w=[f"pgb.{s3}"])
            for j in range(4):
                for cc in range(2):
                    tr(psb[0][:, (j * 2 + cc) * 128:(j * 2 + cc + 1) * 128], pgb[s3][:, j, cc * 128:(cc + 1) * 128], 128, [f"pgb.{s3}"], ["psb0"])
                tr(psb[1][0:96, j * 128:(j + 1) * 128], pgb[s3][:, j, 192:288], 128, [f"pgb.{s3}"], ["psb1"])
            s = slot_of[i]
            T.op("dve", lambda e: e.tensor_copy(out=pgT[s][:, :], in_=psb[0][:, :]), r=["psb0"], w=[f"pgT.{s}"])
            T.op("act", lambda e: e.copy(out=kpT[s][64:96, :], in_=psb[1][64:96, 0:512]), r=["psb1"], w=[f"kpT.{s}"])

        def stA2(i):
            kind_, b, g = items[i]
            ql, qpe = qviews(b)
            sbk = rS.next()
            sbank[i] = sbk
            if kind_ == "grp":
                s = slot_of[i]
                for j in range(4):
                    so = psf[sbk][:, j * 64:(j + 1) * 64]
                    mm(so, pgT[s][:, (j * 2) * 128:(j * 2 + 1) * 128], ql[0], True, False, [f"pgT.{s}", "qlat"], [f"psf{sbk}"])
                    mm(so, pgT[s][:, (j * 2 + 1) * 128:(j * 2 + 2) * 128], ql[1], False, False, [f"pgT.{s}", "qlat"], [f"psf{sbk}"])
                    mm(so, kpT[s][0:96, j * 128:(j + 1) * 128], qpe, False, True, [f"kpT.{s}", "qT.0"], [f"psf{sbk}"])
                T.op("act", lambda e: e.activation(out=Pt[s][:, :], in_=psf[sbk][:, 0:256], func=AF.Exp, scale=SM_SCALE), r=[f"psf{sbk}"], w=[f"Pt.{s}"])
            else:
                so = psf[sbk][:, 0:64]
                mm(so, ckvT[:, 0, 0:128], ql[0], True, False, ["ckvT.0", "qlat"], [f"psf{sbk}"])
                mm(so, ckvT[:, 1, 0:128], ql[1], False, False, ["ckvT.0", "qlat"], [f"psf{sbk}"])
                mm(so, kpTn[0:96, 0:128], qpe, False, True, ["kpTn", "qT.0"], [f"psf{sbk}"])
                T.op("act", lambda e: e.activation(out=Ps[:, :], in_=so, func=AF.Exp, scale=SM_SCALE), r=[f"psf{sbk}"], w=["Ps"])
                T.op("pool", lambda e: e.tensor_tensor(out=Ps[:, :], in0=Ps[:, :], in1=dmask[:, b * 64:(b + 1) * 64], op=ALU.mult), r=["Ps", "dmask"], w=["Ps"])

        def stB(i):
            kind_, b, g = items[i]
            ob = 4 + b % 2
            O = psf[ob]
            if kind_ == "grp":
                s = slot_of[i]
                s3 = slot3[i]
                for j in range(4):
                    mm(O[0:64, 0:289], Pt[s][:, j * 64:(j + 1) * 64], pgb[s3][:, j, 0:289], g == 0 and j == 0, False, [f"Pt.{s}", f"pgb.{s3}"], [f"psf{ob}"])
                return
            mm(O[0:64, 0:289], Ps[:, :], kvb[:, 0, 0:289], NG == 0, True, ["Ps", "kvb.0"], [f"psf{ob}"])
            T.op("dve", lambda e: e.reciprocal(out=rec[0:64, 0:1], in_=O[0:64, 288:289]), r=[f"psf{ob}"], w=["rec"])
            T.op("dve", lambda e: e.tensor_scalar(out=oln[0:64, :], in0=O[0:64, 0:256], scalar1=rec[0:64, 0:1], scalar2=None, op0=ALU.mult), r=[f"psf{ob}", "rec"], w=["oln"])
            bk = rb.next()
            for cc in range(2):
                tr(psb[bk][:, cc * 64:(cc + 1) * 64], oln[0:64, cc * 128:(cc + 1) * 128], 64, ["oln"], [f"psb{bk}"])
            for cc in range(2):
                dstv = olT[:, cc * 1024:(cc + 1) * 1024].rearrange("p (h t) -> p h t", h=8)[:, :, b * 8:(b + 1) * 8]
                srcv = psb[bk][:, cc * 64:(cc + 1) * 64].rearrange("p (h s) -> p h s", h=8)
                if cc == 0:
                    T.op("act", lambda e: e.copy(out=dstv, in_=srcv), r=[f"psb{bk}"], w=["olT"])
                else:
                    T.op("dve", lambda e: e.tensor_copy(out=dstv, in_=srcv), r=[f"psb{bk}"], w=["olT"])

        n_it = len(items)
        for i in range(n_it + 4):
            if i < n_it:
                stG(i)
            if 0 <= i - 2 < n_it:
                stA1(i - 2)
            if 0 <= i - 3 < n_it:
                stA2(i - 3)
            if 0 <= i - 4 < n_it:
                stB(i - 4)
        pv = rf.next()
        for h in range(8):
            for cc in range(2):
                mm(psf[pv][:, h * 64:(h + 1) * 64], olT[:, cc * 1024 + h * 128:cc * 1024 + (h + 1) * 128], wuv[:, cc, h * 64:(h + 1) * 64], cc == 0, cc == 1, ["olT", "wuv"], [f"psf{pv}"])
        T.op("act", lambda e: e.copy(out=otok[:, 0, :], in_=psf[pv][:, 0:512]), r=[f"psf{pv}"], w=["otok.0"])

    try:
        chk("setup")
        order = []
        for seq in range(NSEQ):
            order.append(("meta", seq, 0))
            for ti in range(NT):
                order.append(("prompt", seq, ti))
        order.append(("sample", 0, 0))
        for k, (kd_, sq_, ti_) in enumerate(order[:-1]):
            tile(kd_, sq_, ti_, k % 2, order[k + 1], k == 0)
            chk("meta" if kd_ == "meta" else "tile0")
        chk("prompt")
    except _Stop:
        T.finish("pool")
        return nc, T
    T.barrier()
    cd = Carver()
    dec["pgc32"] = [cd.f32(1024) for _ in range(3)]
    dec["pgk32"] = [cd.f32(128) for _ in range(3)]
    dec["pgb"] = [cd.bf(4 * 290).rearrange("p (j c) -> p j c", j=4) for _ in range(3)]
    dec["pgT"] = [cd.bf(1024) for _ in range(2)]
    dec["kpT"] = [cd.bf(512) for _ in range(2)]
    dec["Pt"] = [cd.bf(256) for _ in range(2)]
    dec["qlat"] = cd.bf(2048)
    dec["olT"] = cd.bf(2048)
    dec["oln"] = cd.bf(256)
    dec["Ps"] = cd.bf(64)
    kpTn = cd.bf(128)
    wukT = cd.bf(8 * 256)[0:64, :].rearrange("p (h c) -> p h c", h=8)
    for cc in range(2):
        bk = rb.next()
        for h in range(8):
            T.op("pe", lambda e: e.transpose(out=psb[bk][0:64, h * 128:(h + 1) * 128], in_=wuk[:, cc, h * 64:(h + 1) * 64], identity=idb[:]),
                 r=["wuk", "idb"], w=[f"psb{bk}"])
        T.op("dve", lambda e: e.tensor_copy(out=wukT[:, :, cc * 128:(cc + 1) * 128], in_=psb[bk][0:64, :].rearrange("p (h c) -> p h c", h=8)),
             r=[f"psb{bk}"], w=["wukT"])
    pts = cd.i32(16 * NPG)
    dec["idx"] = cd.i32(16 * NPG)
    for s in range(3):
        T.op("pool", lambda e: e.memset(dec["pgb"][s][:, :, 288:290], 1.0), w=[f"pgb.{s}"])
    for s in range(2):
        T.op("pool", lambda e: e.memset(dec["kpT"][s][0:64, :], 0.0), w=[f"kpT.{s}"])
    T.op("pool", lambda e: e.memset(kpTn[0:64, :], 0.0), w=["kpTn"])
    T.dma("sp", lambda e: e.dma_start(out=pts, in_=ptab.partition_broadcast(128)), "l_pts", w=["pts"])
    NGT = 16 * NPG // 4
    for q in range(4):
        T.op("dve", lambda e: e.tensor_scalar(out=dec["idx"][32 * q:32 * q + 32, 0:NGT], in0=pts.rearrange("p (G q) -> p G q", q=4)[32 * q:32 * q + 32, :, q],
                                              scalar1=32.0, scalar2=iota[32 * q:32 * q + 32, 0:1], op0=ALU.mult, op1=ALU.add), r=["pts", "iota"], w=["idx"])
    tile("sample", 0, 0, (len(order) - 1) % 2, None, False)
    T.finish("pool")
    return nc, T


_CACHE = {}


def _consts(SEQ, NPG):
    inv = (10000.0 ** (-2.0 * np.arange(16, dtype=np.float32) / 32.0)).astype(np.float32)

    def tab(pos):
        ang = pos.astype(np.float32)[:, None] * inv[None, :]
        return np.concatenate([np.cos(ang), np.sin(ang)], axis=1).astype(np.float32)

    NBLK = SEQ // 128
    rp = np.zeros((128, NBLK * 32), np.float32)
    for j in range(NBLK):
        rp[:, j * 32:(j + 1) * 32] = tab(16 + 128 * j + np.arange(128))
    rm = np.zeros((128, 32), np.float32)
    rm[0:16] = tab(np.arange(16))
    rs = tab(NPG * 128 + (np.arange(128) % 8))
    k = np.arange(128)
    tri = (k[:, None] <= k[None, :]).astype(np.float32)
    dm = np.zeros((128, 16, 8, 8), np.float32)
    for b in range(16):
        for s in range(8):
            dm[b * 8:b * 8 + s + 1, b, :, s] = 1.0
    return {"c_ident": np.eye(128, dtype=np.float32), "c_tri": tri, "c_dmask": dm.reshape(128, 1024),
            "c_rope_p": rp, "c_rope_m": rm, "c_rope_s": rs, "c_iota": (np.arange(128) % 32).astype(np.float32).reshape(128, 1)}


def make_in_maps(inputs, n_cores, NSEQ, SEQ, NPG, NPOOL):
    f = lambda a: np.ascontiguousarray(np.asarray(a, dtype=np.float32))
    shared = {
        "meta": f(inputs["meta_tokens"]),
        "cckv": f(inputs["cache_ckv"]).reshape(NPOOL * 128, 256),
        "ckpe": f(inputs["cache_kpe"]).reshape(NPOOL * 128, 32),
        "lneg": f(inputs["ln_emb_g"]).reshape(1, 1024), "lneb": f(inputs["ln_emb_b"]).reshape(1, 1024),
        "ln1g": f(inputs["ln1_g"]).reshape(1, 1024), "ln1b": f(inputs["ln1_b"]).reshape(1, 1024),
        "ln2g": f(inputs["ln2_g"]).reshape(1, 1024), "ln2b": f(inputs["ln2_b"]).reshape(1, 1024),
        "qng": f(inputs["q_norm_g"]).reshape(1, 384), "kvg": f(inputs["kv_norm_g"]).reshape(1, 256),
        "convw": f(inputs["conv_w"]).reshape(3, 512),
        "w_in": f(inputs["w_in"]).reshape(1024, D_IN), "w_uq": f(inputs["w_uq"]).reshape(384, 768),
        "w_uk": f(inputs["w_uk"]).reshape(256, 512), "w_uv": f(inputs["w_uv"]).reshape(256, 512),
        "w_om": f(inputs["w_o_mla"]).reshape(512, 1024), "w_co": f(inputs["w_conv_out"]).reshape(512, 1024),
        "w_o": f(inputs["w_o"]).reshape(1024, 1024), "w_up": f(inputs["w_up"]).reshape(1024, 4096),
        "w_dn": f(inputs["w_down"]).reshape(4096, 1024),
    }
    shared.update(_consts(SEQ, NPG))
    xp = f(inputs["x_prompt"]); xs = f(inputs["x_sample"]); sc = f(inputs["state_conv"])[0]
    pt = np.ascontiguousarray(np.asarray(inputs["page_table"], dtype=np.int32))
    maps = []
    for c in range(n_cores):
        m = dict(shared)
        m["x_p"] = np.ascontiguousarray(xp[c * NSEQ:(c + 1) * NSEQ])
        m["x_s"] = np.ascontiguousarray(xs[c * 16:(c + 1) * 16]).reshape(128, 1024)
        m["sconv"] = np.ascontiguousarray(sc[c * 16:(c + 1) * 16]).reshape(32, 512)
        m["ptab"] = np.ascontiguousarray(pt[c * 16:(c + 1) * 16]).reshape(1, 16 * NPG)
        maps.append(m)
    return maps


def gather_outputs(results, n_cores, NSEQ, SEQ):
    L = SEQ + 16
    cat = lambda k: np.concatenate([np.asarray(r[k], dtype=np.float32) for r in results], axis=0)
    y_p = cat("y_p").reshape(n_cores * NSEQ, SEQ, 1024)
    y_s = cat("y_s").reshape(n_cores * 16, 8, 1024)
    ckv_p = cat("ckv_p").reshape(1, n_cores * NSEQ, L, 256)
    kpe_p = cat("kpe_p").reshape(1, n_cores * NSEQ, L, 32)
    conv_p = cat("conv_p").reshape(1, n_cores * NSEQ, 2, 512)
    ckv_s = cat("ckv_s").reshape(1, n_cores * 16, 8, 256)
    kpe_s = cat("kpe_s").reshape(1, n_cores * 16, 8, 32)
    conv_s = cat("conv_s").reshape(1, n_cores * 16, 2, 512)
    return (y_p, y_s, ckv_p, kpe_p, conv_p, ckv_s, kpe_s, conv_s)


def kernel(**inputs):
    n_cores = 8
    B, SEQ, _ = inputs["x_prompt"].shape
    NSEQ = B // n_cores
    NPG = inputs["page_table"].shape[1]
    NPOOL = inputs["cache_ckv"].shape[1]
    key = (NSEQ, SEQ, NPG, NPOOL)
    if key not in _CACHE:
        _CACHE[key] = build(NSEQ, SEQ, NPG, NPOOL)[0]
    nc = _CACHE[key]
    maps = make_in_maps(inputs, n_cores, NSEQ, SEQ, NPG, NPOOL)
    res = run_bass_kernel_spmd(nc, maps, core_ids=list(range(n_cores)))
    return gather_outputs(res.results, n_cores, NSEQ, SEQ)
```
